# Optimizing a Trainium2 kernel written in Bass

```python
import jax, jax.numpy as jnp
from jax import lax
import numpy as np

D_MODEL = 1024
BATCH = 16
SEQ = 4096
DEPTH = 2
DEC_BATCH = 2
DEC_SEQ = 16384
PAST_LEN = 128

HEAD_DIM = 64
EPS = 1e-6
NEG = -1e30
A_HEADS = 8
A_KV_HEADS = 2
A_GROUP = A_HEADS // A_KV_HEADS
A_WINDOW = 128
B_PATTERNS = ((128, 1), (512, 4), (2048, 16))
B_HEADS_PER_GROUP = 4
B_HEADS = B_HEADS_PER_GROUP * len(B_PATTERNS)
C_HEADS = 16
C_KV_HEADS = 4
C_GROUP = C_HEADS // C_KV_HEADS
C_BLOCK = 128
ROPE_THETA = 10000.0
GRID_W = 64
D_FF = ((8 * D_MODEL + 3 * 256 - 1) // (3 * 256)) * 256
AB_IN = (A_HEADS + 2 * A_KV_HEADS) * HEAD_DIM + 3 * B_HEADS * HEAD_DIM
AB_OUT = (A_HEADS + B_HEADS_PER_GROUP) * HEAD_DIM
C_IN = (C_HEADS + 2 * C_KV_HEADS) * HEAD_DIM
C_OUT = C_HEADS * HEAD_DIM
N_EVEN = (DEPTH + 1) // 2
N_ODD = DEPTH // 2

kernel_name = "hybrid_window_dilated_axialrope_encoder"


def rmsnorm(x, g):
    xf = x.astype(jnp.float32)
    y = xf * lax.rsqrt(jnp.mean(xf * xf, axis=-1, keepdims=True) + EPS)
    return (y * g.astype(jnp.float32)).astype(x.dtype)


def alibi_slopes(n):
    return jnp.asarray(2.0 ** (-8.0 * np.arange(1, n + 1) / n), dtype=jnp.float32)


def banded_attention(q, k, v, radius, dist_scale, slopes, sink=None):
    n, L, KV, G, dh = q.shape
    blk = radius
    nb = -(-L // blk)
    Lp = nb * blk
    qb = jnp.pad(q, ((0, 0), (0, Lp - L), (0, 0), (0, 0), (0, 0))).reshape(n, nb, blk, KV, G, dh)
    kv_pad = ((0, 0), (blk, Lp - L + blk), (0, 0), (0, 0))
    kp = jnp.pad(k, kv_pad).reshape(n, nb + 2, blk, KV, dh)
    vp = jnp.pad(v, kv_pad).reshape(n, nb + 2, blk, KV, dh)
    kw = jnp.concatenate([kp[:, :-2], kp[:, 1:-1], kp[:, 2:]], axis=2)
    vw = jnp.concatenate([vp[:, :-2], vp[:, 1:-1], vp[:, 2:]], axis=2)
    s = jnp.einsum('nbqhgd,nbkhd->nbhgqk', qb, kw, preferred_element_type=jnp.float32) * (dh ** -0.5)
    rel = jnp.arange(3 * blk)[None, :] - blk - jnp.arange(blk)[:, None]
    key_pos = jnp.arange(nb)[:, None] * blk - blk + jnp.arange(3 * blk)[None, :]
    valid = (jnp.abs(rel) <= radius)[None] & ((key_pos >= 0) & (key_pos < L))[:, None, :]
    bias = -(slopes.astype(jnp.float32) * dist_scale)[:, :, None, None] * jnp.abs(rel).astype(jnp.float32)
    s = jnp.where(valid[None, :, None, None], s + bias, NEG)
    m = jnp.max(s, axis=-1)
    if sink is not None:
        sink_f = sink.astype(jnp.float32)[:, :, None]
        m = jnp.maximum(m, sink_f)
    p = jnp.exp(s - m[..., None])
    den = jnp.sum(p, axis=-1)
    if sink is not None:
        den = den + jnp.exp(sink_f - m)
    o = jnp.einsum('nbhgqk,nbkhd->nbqhgd', p.astype(v.dtype), vw, preferred_element_type=jnp.float32)
    o = o / jnp.moveaxis(den, -1, 2)[..., None]
    lse = jnp.moveaxis(m + jnp.log(den), -1, 2)
    return o.reshape(n, Lp, KV, G, dh)[:, :L], lse.reshape(n, Lp, KV, G)[:, :L]


def dilated_attention(q, k, v, window, dil, slopes):
    n, T, _ = q.shape
    H = B_HEADS_PER_GROUP
    Ls = T // dil

    def by_stride(x):
        return x.reshape(n, Ls, dil, H, HEAD_DIM).transpose(0, 2, 1, 3, 4).reshape(n * dil, Ls, H, HEAD_DIM)

    o, lse = banded_attention(by_stride(q)[:, :, :, None], by_stride(k), by_stride(v),
                              window // (2 * dil), dil, slopes)
    o = o.reshape(n, dil, Ls, H, HEAD_DIM).transpose(0, 2, 1, 3, 4).reshape(n, T, H, HEAD_DIM)
    lse = lse.reshape(n, dil, Ls, H).transpose(0, 2, 1, 3).reshape(n, T, H)
    return o, lse


def mixer_ab(h, w_in, w_out, sink):
    n, T, _ = h.shape
    proj = h @ w_in
    dq, dkv, db = A_HEADS * HEAD_DIM, A_KV_HEADS * HEAD_DIM, B_HEADS_PER_GROUP * HEAD_DIM
    qa = proj[..., :dq].reshape(n, T, A_KV_HEADS, A_GROUP, HEAD_DIM)
    ka = proj[..., dq:dq + dkv].reshape(n, T, A_KV_HEADS, HEAD_DIM)
    va = proj[..., dq + dkv:dq + 2 * dkv].reshape(n, T, A_KV_HEADS, HEAD_DIM)
    oa, _ = banded_attention(qa, ka, va, A_WINDOW, 1,
                             alibi_slopes(A_HEADS).reshape(A_KV_HEADS, A_GROUP),
                             sink.reshape(A_KV_HEADS, A_GROUP))
    oa = oa.reshape(n, T, dq).astype(h.dtype)
    slopes_b = alibi_slopes(B_HEADS).reshape(len(B_PATTERNS), B_HEADS_PER_GROUP, 1)
    base0 = dq + 2 * dkv
    outs, lses = [], []
    for i, (window, dil) in enumerate(B_PATTERNS):
        base = base0 + i * 3 * db
        o, lse = dilated_attention(proj[..., base:base + db], proj[..., base + db:base + 2 * db],
                                   proj[..., base + 2 * db:base + 3 * db], window, dil, slopes_b[i])
        outs.append(o)
        lses.append(lse)
    wts = jax.nn.softmax(jnp.stack(lses, axis=0), axis=0)
    ob = jnp.sum(wts[..., None] * jnp.stack(outs, axis=0), axis=0).reshape(n, T, db).astype(h.dtype)
    return jnp.concatenate([oa, ob], axis=-1) @ w_out


def rope_2d(x, rows):
    n_freq = HEAD_DIM // 4
    inv = ROPE_THETA ** (-jnp.arange(n_freq, dtype=jnp.float32) / n_freq)
    row = jnp.repeat(jnp.arange(rows, dtype=jnp.float32), GRID_W)
    col = jnp.tile(jnp.arange(GRID_W, dtype=jnp.float32), rows)
    ang = jnp.concatenate([row[:, None] * inv, col[:, None] * inv], axis=-1)
    c, s = jnp.cos(ang)[:, None, :], jnp.sin(ang)[:, None, :]
    xr = x.astype(jnp.float32).reshape(x.shape[:-1] + (HEAD_DIM // 2, 2))
    x0, x1 = xr[..., 0], xr[..., 1]
    out = jnp.stack([x0 * c - x1 * s, x0 * s + x1 * c], axis=-1)
    return out.reshape(x.shape).astype(x.dtype)


def mixer_c(h, w_in, w_out, q_gain, k_gain):
    n, T, _ = h.shape
    rows = T // GRID_W
    proj = h @ w_in
    dq, dkv = C_HEADS * HEAD_DIM, C_KV_HEADS * HEAD_DIM
    q = proj[..., :dq].reshape(n, T, C_HEADS, HEAD_DIM)
    k = proj[..., dq:dq + dkv].reshape(n, T, C_KV_HEADS, HEAD_DIM)
    v = proj[..., dq + dkv:].reshape(n, T, C_KV_HEADS, HEAD_DIM)
    q = rope_2d(rmsnorm(q, q_gain), rows)
    k = rope_2d(rmsnorm(k, k_gain), rows)
    nb = T // C_BLOCK
    qb = q.reshape(n, nb, C_BLOCK, C_KV_HEADS, C_GROUP, HEAD_DIM).transpose(1, 0, 2, 3, 4, 5)
    scale = HEAD_DIM ** -0.5

    def attend_block(qblk):
        s = jnp.einsum('nqhgd,nkhd->nhgqk', qblk, k, preferred_element_type=jnp.float32) * scale
        p = jax.nn.softmax(s, axis=-1)
        return jnp.einsum('nhgqk,nkhd->nqhgd', p.astype(v.dtype), v,
                          preferred_element_type=jnp.float32).astype(v.dtype)

    o = lax.map(attend_block, qb)
    o = o.transpose(1, 0, 2, 3, 4, 5).reshape(n, T, C_OUT)
    return o @ w_out


def swiglu(h, w1, w3, w2):
    return (jax.nn.silu(h @ w1) * (h @ w3)) @ w2


def trunk(x, norm_mix, w_in_ab, w_out_ab, sink_a, w_in_c, w_out_c, q_gain_c, k_gain_c,
          norm_ffn, ffn_w1, ffn_w3, ffn_w2, final_norm):
    for layer in range(DEPTH):
        h = rmsnorm(x, norm_mix[layer])
        i = layer // 2
        if layer % 2 == 0:
            h = mixer_ab(h, w_in_ab[i], w_out_ab[i], sink_a[i])
        else:
            h = mixer_c(h, w_in_c[i], w_out_c[i], q_gain_c[i], k_gain_c[i])
        x = x + h
        x = x + swiglu(rmsnorm(x, norm_ffn[layer]), ffn_w1[layer], ffn_w3[layer], ffn_w2[layer])
    return rmsnorm(x, final_norm)


def setup_inputs(seed: int = 0) -> dict:
    key = jax.random.key(seed)
    ks = jax.random.split(key, 16)
    f32 = jnp.float32
    nrm = lambda k, shape, scale: jax.random.normal(k, shape, f32) * scale
    return {
        "x_prompt": nrm(ks[0], (BATCH, SEQ, D_MODEL), 1.0),
        "x_sample": nrm(ks[1], (DEC_BATCH, DEC_SEQ, D_MODEL), 1.0),
        "norm_mix": 1.0 + nrm(ks[2], (DEPTH, D_MODEL), 0.05),
        "w_in_ab": nrm(ks[3], (N_EVEN, D_MODEL, AB_IN), D_MODEL ** -0.5),
        "w_out_ab": nrm(ks[4], (N_EVEN, AB_OUT, D_MODEL), AB_OUT ** -0.5),
        "sink_a": nrm(ks[5], (N_EVEN, A_HEADS), 1.0),
        "w_in_c": nrm(ks[6], (N_ODD, D_MODEL, C_IN), D_MODEL ** -0.5),
        "w_out_c": nrm(ks[7], (N_ODD, C_OUT, D_MODEL), C_OUT ** -0.5),
        "q_gain_c": 1.0 + nrm(ks[8], (N_ODD, HEAD_DIM), 0.05),
        "k_gain_c": 1.0 + nrm(ks[9], (N_ODD, HEAD_DIM), 0.05),
        "norm_ffn": 1.0 + nrm(ks[10], (DEPTH, D_MODEL), 0.05),
        "ffn_w1": nrm(ks[11], (DEPTH, D_MODEL, D_FF), D_MODEL ** -0.5),
        "ffn_w3": nrm(ks[12], (DEPTH, D_MODEL, D_FF), D_MODEL ** -0.5),
        "ffn_w2": nrm(ks[13], (DEPTH, D_FF, D_MODEL), D_FF ** -0.5),
        "final_norm": 1.0 + nrm(ks[14], (D_MODEL,), 0.05),
    }


def reference(x_prompt, x_sample, norm_mix, w_in_ab, w_out_ab, sink_a, w_in_c, w_out_c,
              q_gain_c, k_gain_c, norm_ffn, ffn_w1, ffn_w3, ffn_w2, final_norm):
    y_prompt = trunk(x_prompt, norm_mix, w_in_ab, w_out_ab, sink_a, w_in_c, w_out_c, q_gain_c,
                     k_gain_c, norm_ffn, ffn_w1, ffn_w3, ffn_w2, final_norm)
    y_sample = trunk(x_sample, norm_mix, w_in_ab, w_out_ab, sink_a, w_in_c, w_out_c, q_gain_c,
                     k_gain_c, norm_ffn, ffn_w1, ffn_w3, ffn_w2, final_norm)
    return (y_prompt, y_sample)
```

```python
import contextlib
import numpy as np
import concourse.bass as bass
import concourse.mybir as mybir
from concourse.bass_utils import run_bass_kernel_spmd

F32 = mybir.dt.float32
BF16 = mybir.dt.bfloat16
AF = mybir.ActivationFunctionType
ALU = mybir.AluOpType
AX = mybir.AxisListType

D = 1024
DFF = 2816
NJ = 22
SEGL = 4096
NT0 = 15360
SEG_START = [1024, 6144, 10240]
OWN = {0: 0, 1: 4096, 2: 8192}
NC_TOK = 12288
NOWN = 12288
EPS = 1e-6
NEGM = -30000.0
DEBUG_OUT = ("O0s", "X1s", "Q1s", "O1s")


class Res:
    __slots__ = ("name", "w", "rd")

    def __init__(self, name=""):
        self.name = name
        self.w = None
        self.rd = {}


class FW:
    NDMA = 8

    def __init__(self, nc, stack):
        self.nc = nc
        self.engs = {"pe": nc.tensor, "act": nc.scalar, "dve": nc.vector, "pool": nc.gpsimd, "sp": nc.sync}
        self.sems = {}
        self.cnt = {}
        for e in self.engs:
            self.sems[e] = stack.enter_context(nc.semaphore("s_" + e))
            self.cnt[e] = 0
        self.dsem, self.dcnt, self.dnext = {}, {}, {}
        for q in ("sp", "pool", "act"):
            self.dsem[q] = [stack.enter_context(nc.semaphore(f"d_{q}{i}")) for i in range(self.NDMA)]
            self.dcnt[q] = [0] * self.NDMA
            self.dnext[q] = 0
        self.known = {e: {} for e in self.engs}
        self.ninstr = 0
        self.sems["cc"] = stack.enter_context(nc.semaphore("s_cc"))
        self.cnt["cc"] = 0

    def _sem(self, key):
        if isinstance(key, tuple):
            return self.dsem[key[0]][key[1]]
        return self.sems[key]

    def _wait(self, eng, key, val):
        k = self.known[eng]
        if k.get(key, 0) >= val:
            return
        self.engs[eng].wait_ge(self._sem(key), val)
        k[key] = val
        self.ninstr += 1

    def _deps(self, eng, reads, writes, selfsync=True):
        for r in reads:
            if r.w is not None:
                key, val, weng = r.w
                if weng != eng or (selfsync and eng != "pe"):
                    self._wait(eng, key, val)
        for w in writes:
            if w.w is not None:
                key, val, weng = w.w
                if weng != eng or (selfsync and eng != "pe"):
                    self._wait(eng, key, val)
            for reng, (key, val) in w.rd.items():
                if reng != eng or (selfsync and eng != "pe"):
                    self._wait(eng, key, val)

    def op(self, eng, fn, reads=(), writes=(), inc=True):
        self._deps(eng, reads, writes)
        ins = fn()
        self.ninstr += 1
        if inc:
            self.cnt[eng] += 1
            ins.then_inc(self.sems[eng], 1)
            val = self.cnt[eng]
        else:
            val = self.cnt[eng] + 1
        for r in reads:
            r.rd[eng] = (eng, val)
        for w in writes:
            w.w = (eng, val, eng)
            w.rd = {}
        return ins

    def dma(self, q, out, in_, reads=(), writes=(), **kw):
        i = self.dnext[q]
        self.dnext[q] = (i + 1) % self.NDMA
        if self.dcnt[q][i] > 0:
            self._wait(q, (q, i), self.dcnt[q][i])
        self._deps(q, reads, writes, selfsync=True)
        self.dcnt[q][i] += 16
        val = self.dcnt[q][i]
        ins = self.engs[q].dma_start(out=out, in_=in_, **kw)
        ins.then_inc(self.dsem[q][i], 16)
        self.ninstr += 1
        key = (q, i)
        for r in reads:
            r.rd["dma_%s%d" % (q, i)] = (key, val)
        for w in writes:
            w.w = (key, val, "dma")
            w.rd = {}
        return ins

    def coll(self, kind, groups, in_ap, out_ap, out_res):
        ins = self.nc.gpsimd.collective_compute(kind, mybir.AluOpType.bypass, replica_groups=groups, ins=[in_ap], outs=[out_ap])
        ins.then_inc(self.sems["cc"])
        self.cnt["cc"] += 1
        self.ninstr += 1
        out_res.w = ("cc", self.cnt["cc"], "cc")
        out_res.rd = {}

    def barrier(self):
        for e in self.engs:
            if self.cnt["cc"] > 0:
                self._wait(e, "cc", self.cnt["cc"])
            for e2 in self.engs:
                if e2 != e and self.cnt[e2] > 0:
                    self._wait(e, e2, self.cnt[e2])
            for q in self.dsem:
                for i in range(self.NDMA):
                    if self.dcnt[q][i] > 0:
                        self._wait(e, (q, i), self.dcnt[q][i])


class RD(dict):
    def __missing__(self, k):
        r = Res(str(k))
        self[k] = r
        return r


def build_program(debug=False):
    nc = bass.Bass("TRN2", target_bir_lowering=False)

    def din(name, shape, dt=F32):
        return nc.dram_tensor(name, list(shape), dt, kind="ExternalInput").ap()

    def dscr(name, shape, dt):
        kind = "ExternalOutput" if (debug and name in DEBUG_OUT) else "Internal"
        return nc.dram_tensor(name, list(shape), dt, kind=kind).ap()

    xin = din("xin", [NT0, D])
    masks = din("masks", [128, 12])
    biasA_d = din("biasA", [128, 8 * 384])
    biasB_d = din("biasB", [128, 12 * 512])
    cos_d = din("cosT", [NC_TOK, 32])
    sin_d = din("sinT", [NC_TOK, 32])
    ident_d = din("ident", [128, 128])
    norm_mix = din("norm_mix", [2, D])
    w_in_ab = din("w_in_ab", [D, 3072])
    w_out_ab = din("w_out_ab", [768, D])
    sink_a = din("sink_a", [1, 8])
    w_in_c = din("w_in_c", [D, 1536])
    w_out_c = din("w_out_c", [D, D])
    q_gain = din("q_gain_c", [1, 64])
    k_gain = din("k_gain_c", [1, 64])
    norm_ffn = din("norm_ffn", [2, D])
    ffn_w1 = din("ffn_w1", [2, D, DFF])
    ffn_w3 = din("ffn_w3", [2, D, DFF])
    ffn_w2 = din("ffn_w2", [2, DFF, D])
    final_norm = din("final_norm", [1, D])
    yout = nc.dram_tensor("yout", [NOWN, D], F32, kind="ExternalOutput").ap()

    W13s = dscr("W13s", [2, NJ, 128, 8, 256], BF16)
    QK0s = dscr("QK0s", [18, 128, NT0], BF16)
    V0s = dscr("V0s", [NT0, 896], BF16)
    O0s = dscr("O0s", [6, 128, NC_TOK], BF16)
    X1s = dscr("X1s", [NOWN, D], F32)
    Q1s = dscr("Q1s", [8, 128, NOWN], BF16)
    K1samp = [dscr("K1samp%d" % g, [64, 4096], BF16) for g in range(4)]
    V1samp = [dscr("V1samp%d" % g, [4096, 64], BF16) for g in range(4)]
    K1g = [nc.dram_tensor("K1g%d" % g, [256, 4096], BF16, kind="Internal", addr_space="Local").ap() for g in range(4)]
    V1g = [nc.dram_tensor("V1g%d" % g, [16384, 64], BF16, kind="Internal", addr_space="Local").ap() for g in range(4)]
    K1p = dscr("K1p", [4, 64, 8192], BF16)
    V1p = dscr("V1p", [8192, 256], BF16)
    O1s = dscr("O1s", [8, 128, NOWN], BF16)

    with contextlib.ExitStack() as top:
        fw = FW(nc, top)
        R = RD()
        PA = top.enter_context(nc.psum_tensor("PA", [128, 1024], F32))
        PB = top.enter_context(nc.psum_tensor("PB", [128, 1024], F32))
        banks = [PA[:, 0:512], PA[:, 512:1024], PB[:, 0:512], PB[:, 512:1024]]
        banks += [top.enter_context(nc.psum_tensor("bank%d" % i, [128, 512], F32))[:, :] for i in range(4, 7)]
        pT = top.enter_context(nc.psum_tensor("pT", [128, 1024], BF16))
        BK = [R["bank%d" % i] for i in range(7)]
        RpT = R["pT"]

        def sbuf(st, name, shape, dt):
            return st.enter_context(nc.sbuf_tensor(name, list(shape), dt))

        identF = sbuf(top, "identF", [128, 128], F32)
        identB = sbuf(top, "identB", [128, 128], BF16)
        maskt = sbuf(top, "maskt", [128, 12], F32)
        fw.dma("sp", identF[:], ident_d, writes=[R["identF"]])
        fw.dma("sp", maskt[:], masks, writes=[R["maskt"]])
        fw.op("dve", lambda: nc.vector.tensor_copy(out=identB[:], in_=identF[:]), [R["identF"]], [R["identB"]])

        rr = {"cp": 0}

        def copy_rr(out, in_, reads, writes, engines=("act", "dve")):
            e = engines[rr["cp"] % len(engines)]
            rr["cp"] += 1
            if e == "act":
                fw.op("act", lambda: nc.scalar.copy(out=out, in_=in_), reads, writes)
            elif e == "dve":
                fw.op("dve", lambda: nc.vector.tensor_copy(out=out, in_=in_), reads, writes)
            else:
                fw.op("pool", lambda: nc.gpsimd.tensor_copy(out=out, in_=in_), reads, writes)

        def mm(out, lhsT, rhs, start, stop, reads, writes):
            fw.op("pe", lambda: nc.tensor.matmul(out, lhsT=lhsT, rhs=rhs, start=start, stop=stop),
                  reads, writes, inc=stop)

        def load_cast(st_tile, st_res, src_ap, width, pieces, gain_ap=None, eng_rr=("dve", "pool")):
            fw.dma("sp", st_tile[:, 0:width], src_ap, writes=[st_res])
            for n, (sc, dst, w, dres) in enumerate(pieces):
                e = eng_rr[n % len(eng_rr)]
                if gain_ap is None:
                    if e == "dve":
                        fw.op("dve", lambda: nc.vector.tensor_copy(out=dst, in_=st_tile[:, sc:sc + w]), [st_res], [dres])
                    else:
                        fw.op("pool", lambda: nc.gpsimd.tensor_copy(out=dst, in_=st_tile[:, sc:sc + w]), [st_res], [dres])
                else:
                    engh = nc.vector if e == "dve" else nc.gpsimd
                    fw.op(e, lambda: engh.tensor_scalar(out=dst, in0=st_tile[:, sc:sc + w], scalar1=gain_ap,
                                                         scalar2=None, op0=ALU.mult),
                          [st_res, R["gains"]], [dres])

        gains = sbuf(top, "gains", [128, 4, 8], F32)
        for l in range(2):
            fw.dma("sp", gains[:, l, :], norm_mix[l:l + 1, :].rearrange("o (kc p) -> p (o kc)", p=128), writes=[R["gains"]], allow_slow_non_contiguous=True)
            fw.dma("sp", gains[:, 2 + l, :], norm_ffn[l:l + 1, :].rearrange("o (kc p) -> p (o kc)", p=128), writes=[R["gains"]], allow_slow_non_contiguous=True)
        with contextlib.ExitStack() as ph:
            stg = [sbuf(ph, "p0stg%d" % i, [128, DFF], F32) for i in range(2)]
            wst = [sbuf(ph, "p0w%d" % i, [128, NJ, 256], BF16) for i in range(2)]
            n = 0
            for l in range(2):
                for kc in range(8):
                    ws = wst[n % 2]
                    wres = R["p0w%d" % (n % 2)]
                    for wi, wsrc in enumerate((ffn_w1, ffn_w3)):
                        s = stg[wi]
                        sres = R["p0stg%d" % wi]
                        fw.dma("sp", s[:], wsrc[l, kc * 128:(kc + 1) * 128, :], writes=[sres])
                        e = "dve" if wi == 0 else "pool"
                        engh = nc.vector if wi == 0 else nc.gpsimd
                        fw.op(e, lambda: engh.tensor_scalar(
                            out=ws[:, :, wi * 128:(wi + 1) * 128],
                            in0=s[:].rearrange("p (j c) -> p j c", c=128),
                            scalar1=gains[:, 2 + l, kc:kc + 1], scalar2=None, op0=ALU.mult),
                            [sres, R["gains"]], [wres])
                    fw.dma("pool", W13s[l, :, :, kc, :].rearrange("j p c -> p j c"), ws[:], reads=[wres],
                           writes=[R["W13s"]])
                    n += 1
        fw.barrier()

        def norm_transpose(xt_ap, xres, ss_t, ssres, col, xs_t, xsres, xnT, xnres, tcol):
            raise NotImplementedError

        def sumsq(x_ap, xres, junk, junkres, ss_ap, ssres):
            fw.op("act", lambda: nc.scalar.activation(out=junk, in_=x_ap, func=AF.Square, accum_out=ss_ap),
                  [xres], [junkres, ssres])

        def rstd_from_ss(ss_ap, ssres, rs_ap, rsres, n):
            fw.op("act", lambda: nc.scalar.activation(out=rs_ap, in_=ss_ap, func=AF.Sqrt, bias=epst[:, 0:1], scale=1.0 / n),
                  [ssres, R["epst"]], [rsres])
            fw.op("dve", lambda: nc.vector.reciprocal(out=rs_ap, in_=rs_ap), [rsres], [rsres])

        epst = sbuf(top, "epst", [128, 1], F32)
        fw.op("pool", lambda: nc.gpsimd.memset(epst[:], EPS), [], [R["epst"]])

        def scale_only(x_ap, xres, rs_ap, rsres, xs_t, xsres):
            fw.op("dve", lambda: nc.vector.tensor_scalar(out=xs_t[:], in0=x_ap, scalar1=rs_ap, scalar2=None, op0=ALU.mult),
                  [xres, rsres], [xsres])

        def scale_transpose(x_ap, xres, rs_ap, rsres, xs_t, xsres, xnT, xnres, tok0):
            scale_only(x_ap, xres, rs_ap, rsres, xs_t, xsres)
            transpose_only(xs_t, xsres, xnT, xnres, tok0)

        pT2 = banks[6].bitcast(BF16)
        RpTh = [RpT, BK[6]]
        pTh = [pT, pT2]

        def transpose_only(xs_t, xsres, xnT, xnres, tok0):
            for half in range(2):
                for kc in range(4 * half, 4 * half + 4):
                    fw.op("pe", lambda: nc.tensor.transpose(out=pTh[half][:, (kc % 4) * 128:(kc % 4 + 1) * 128],
                                                             in_=xs_t[:, kc * 128:(kc + 1) * 128], identity=identB[:]),
                          [xsres, R["identB"]], [RpTh[half]], inc=(kc % 4 == 3))
                src = pTh[half][:, 0:512].rearrange("p (k t) -> p k t", k=4)
                dst = xnT[:, 4 * half:4 * half + 4, tok0:tok0 + 128]
                if half == 0:
                    fw.op("act", lambda: nc.scalar.copy(out=dst, in_=src), [RpTh[half]], [xnres])
                else:
                    fw.op("dve", lambda: nc.vector.tensor_copy(out=dst, in_=src), [RpTh[half]], [xnres])

        with contextlib.ExitStack() as ph:
            Wab = sbuf(ph, "Wab", [128, 8, 3200], BF16)
            stg = sbuf(ph, "p1stg", [128, 3072], F32)
            for kc in range(8):
                pieces = []
                g = gains[:, 0, kc:kc + 1]

                def pc(sc, dc, w):
                    pieces.append((sc, Wab[:, kc, dc:dc + w], w, R["Wab"]))
                pc(0, 0, 512)
                pc(512, 512, 64); pc(512, 576, 64); pc(576, 640, 64); pc(576, 704, 64)
                for i3 in range(3):
                    pc(768 + 768 * i3, 768 + 512 * i3, 512)
                pc(640, 2304, 128)
                for i3 in range(3):
                    pc(768 + 768 * i3 + 512, 2432 + 256 * i3, 256)
                load_cast(stg, R["p1stg"], w_in_ab[kc * 128:(kc + 1) * 128, :], 3072, pieces, gain_ap=g)
            xg = [sbuf(ph, "p1xg%d" % i, [128, 4, D], F32) for i in range(2)]
            junk = sbuf(ph, "p1junk", [128, D], BF16)
            xs = [sbuf(ph, "p1xs%d" % i, [128, D], BF16) for i in range(12)]
            ssq = [sbuf(ph, "p1ss%d" % i, [128, 4], F32) for i in range(3)]
            xnT = [sbuf(ph, "p1xnT%d" % i, [128, 8, 512], BF16) for i in range(2)]
            qst = [sbuf(ph, "p1qst%d" % i, [128, 3, 512], BF16) for i in range(3)]
            vst = [sbuf(ph, "p1vst%d" % i, [128, 4, 896], BF16) for i in range(2)]
            K_CHUNKS = [4, 5, 8, 9, 12, 13, 16, 17]
            pad_groups = {0, 1, 10, 11, 28, 29}
            cnt1 = {"nb": 0, "nq": 0}
            NG1 = NT0 // 512

            def p1_s1(gi):
                sq = ssq[gi % 3]
                sqr = R["p1ss%d" % (gi % 3)]
                xgt = xg[gi % 2]
                xr = R["p1xg%d" % (gi % 2)]
                fw.dma("sp", xgt[:, :, :], xin[gi * 512:(gi + 1) * 512, :].rearrange("(t p) f -> p t f", p=128), writes=[xr])
                for t in range(4):
                    sumsq(xgt[:, t, :], xr, junk[:], R["p1junk"], sq[:, t:t + 1], sqr)
                    rstd_from_ss(sq[:, t:t + 1], sqr, sq[:, t:t + 1], sqr, D)
                    xi = (gi % 3) * 4 + t
                    scale_only(xgt[:, t, :], xr, sq[:, t:t + 1], sqr, xs[xi], R["p1xs%d" % xi])

            def p1_s2(gi):
                for t in range(4):
                    xi = (gi % 3) * 4 + t
                    transpose_only(xs[xi], R["p1xs%d" % xi], xnT[gi % 2], R["p1xnT%d" % (gi % 2)], t * 128)

            def p1_s2_tile(gi, t):
                xi = (gi % 3) * 4 + t
                transpose_only(xs[xi], R["p1xs%d" % xi], xnT[gi % 2], R["p1xnT%d" % (gi % 2)], t * 128)

            def p1_s3(gi, nxt=None):
                xn = xnT[gi % 2]
                xnr = R["p1xnT%d" % (gi % 2)]
                chunks = K_CHUNKS if gi in pad_groups else list(range(18))
                step = 4 if len(chunks) == 18 else 2
                runs = []
                for c in chunks:
                    if runs and runs[-1][-1] == c - 1 and len(runs[-1]) < 3:
                        runs[-1].append(c)
                    else:
                        runs.append([c])
                ci = 0
                for run in runs:
                    qs = qst[cnt1["nq"] % 3]
                    qsr = R["p1qst%d" % (cnt1["nq"] % 3)]
                    cnt1["nq"] += 1
                    for k_, c in enumerate(run):
                        if nxt is not None and ci % step == 1 and ci // step < 4:
                            p1_s2_tile(nxt, ci // step)
                        ci += 1
                        bk = cnt1["nb"] % 4
                        cnt1["nb"] += 1
                        for kc in range(8):
                            mm(banks[bk][:, :], Wab[:, kc, c * 128:(c + 1) * 128], xn[:, kc, :], kc == 0, kc == 7,
                               [R["Wab"], xnr], [BK[bk]])
                        copy_rr(qs[:, k_, :], banks[bk][:, :], [BK[bk]], [qsr])
                    fw.dma("pool", QK0s[run[0]:run[0] + len(run), :, gi * 512:(gi + 1) * 512].rearrange("c p t -> p c t"),
                           qs[:, 0:len(run), :], reads=[qsr], writes=[R["QK0s"]])
                vs = vst[gi % 2]
                vsr = R["p1vst%d" % (gi % 2)]
                for t in range(4):
                    for (c0, w) in ((0, 512), (512, 384)):
                        bk = 4 + (cnt1["nb"] % 2)
                        cnt1["nb"] += 1
                        for kc in range(8):
                            mm(banks[bk][:, 0:w], xn[:, kc, t * 128:(t + 1) * 128], Wab[:, kc, 2304 + c0:2304 + c0 + w],
                               kc == 0, kc == 7, [R["Wab"], xnr], [BK[bk]])
                        copy_rr(vs[:, t, c0:c0 + w], banks[bk][:, 0:w], [BK[bk]], [vsr])
                fw.dma("pool", V0s[gi * 512:(gi + 1) * 512, :].rearrange("(t p) f -> p t f", p=128), vs[:, :, :], reads=[vsr],
                       writes=[R["V0s"]])

            p1_s1(0)
            p1_s1(1)
            p1_s2(0)
            for gi in range(NG1):
                if gi + 2 < NG1:
                    p1_s1(gi + 2)
                p1_s3(gi, nxt=(gi + 1 if gi + 1 < NG1 else None))
        fw.barrier()

        class Pipe:
            def __init__(self):
                self.steps = []
                self.deferred = []
                self.tick = 0

            def defer(self, delay, fn):
                self.deferred.append([self.tick + delay, fn])

            def run_deferred(self, flush=False):
                keep = []
                for it in self.deferred:
                    if flush or it[0] <= self.tick:
                        it[1]()
                    else:
                        keep.append(it)
                self.deferred = keep

            def add(self, qk_fn, pv_fn):
                self.steps.append((qk_fn, pv_fn))

            def run(self, look=2):
                n = len(self.steps)
                for k in range(n + look):
                    if k < n:
                        self.steps[k][0]()
                    if k >= look:
                        self.steps[k - look][1]()
                    self.tick += 1
                    self.run_deferred()
                self.steps = []

            def flush(self):
                while self.deferred:
                    self.tick += 1
                    self.run_deferred()

        with contextlib.ExitStack() as ph:
            biasA = sbuf(ph, "biasA_sb", [128, 8 * 384], BF16)
            biasB = sbuf(ph, "biasB_sb", [128, 12 * 512], BF16)
            Kbuf = [sbuf(ph, "Kbuf%d" % i, [128, 2, 6144], BF16) for i in range(2)]
            stgf = Kbuf[0][:, :, :].rearrange("p a b -> p (a b)").bitcast(F32)
            fw.dma("sp", stgf[:, 0:8 * 384], biasA_d, writes=[R["p2stgf"]])
            fw.op("dve", lambda: nc.vector.tensor_copy(out=biasA[:], in_=stgf[:, 0:8 * 384]), [R["p2stgf"]], [R["biasA"]])
            for hb in range(2):
                fw.dma("sp", stgf[:, 0:3072], biasB_d[:, hb * 3072:(hb + 1) * 3072], writes=[R["p2stgf"]])
                fw.op("dve", lambda: nc.vector.tensor_copy(out=biasB[:, hb * 3072:(hb + 1) * 3072], in_=stgf[:, 0:3072]), [R["p2stgf"]], [R["biasB"]])
            es = sbuf(ph, "es", [128, 8], F32)
            fw.dma("sp", es[:], sink_a.partition_broadcast(128), writes=[R["es"]])
            fw.op("act", lambda: nc.scalar.activation(out=es[:], in_=es[:], func=AF.Exp), [R["es"]], [R["es"]])

            Qbuf = [sbuf(ph, "Qbuf%d" % i, [128, 4096], BF16) for i in range(3)]
            for i in range(2):
                fw.op("pool", lambda: nc.gpsimd.memset(Kbuf[i][64:128, 0, :], 0.0), [R["biasB"], R["biasA"]], [R["Kbuf%d" % i], R["p2stgf"]])
                fw.op("pool", lambda: nc.gpsimd.memset(Kbuf[i][0:64, 1, :], 0.0), [R["biasB"], R["biasA"]], [R["Kbuf%d" % i], R["p2stgf"]])
            Vbuf = [sbuf(ph, "Vbuf%d" % i, [128, 2, 48, 128], BF16) for i in range(2)]
            acc = [sbuf(ph, "acc%d" % i, [128, 4096], F32) for i in range(2)]
            PT = sbuf(ph, "PT", [128, 8, 512], BF16)
            ost = [sbuf(ph, "ost%d" % i, [128, 512], BF16) for i in range(4)]
            Dt = [sbuf(ph, "Dt%d" % i, [128, 512], F32) for i in range(3)]
            RDs = [sbuf(ph, "RDs%d" % i, [128, 512], F32) for i in range(3)]
            Ot = [sbuf(ph, "Ot%d" % i, [128, 512], F32) for i in range(3)]
            for i in range(2):
                for par in range(2):
                    fw.op("pool", lambda: nc.gpsimd.memset(Vbuf[i][:, par, :, (1 - par) * 64:(1 - par) * 64 + 64], 1.0),
                          [], [R["Vbuf%d_%d" % (i, par)]])
            pipe = Pipe()
            ctr = {"st": 0, "pt": 0, "ob": 0, "fin": 0, "ost": 0, "k": 0, "q": 0, "v": 0}

            def finalize(src_ap, src_res, width, par, es_ap, dst_dram, release_first=True):
                oh, dh = (0, 64) if par == 0 else (64, 0)
                f = ctr["fin"] % 3
                ctr["fin"] += 1
                dt, rd, ot = Dt[f], RDs[f], Ot[f]
                dtr, rdr, otr = R["Dt%d" % f], R["RDs%d" % f], R["Ot%d" % f]
                if es_ap is not None:
                    fw.op("act", lambda: nc.scalar.activation(out=dt[dh:dh + 64, 0:width], in_=src_ap[dh:dh + 64, 0:width], func=AF.Ln,
                                                               bias=es_ap[dh:dh + 64, :], scale=1.0),
                          [src_res, R["es"]], [dtr])
                else:
                    fw.op("act", lambda: nc.scalar.activation(out=dt[dh:dh + 64, 0:width], in_=src_ap[dh:dh + 64, 0:width], func=AF.Ln),
                          [src_res], [dtr])
                fw.op("dve", lambda: nc.vector.tensor_copy(out=ot[oh:oh + 64, 0:width], in_=src_ap[oh:oh + 64, 0:width]),
                      [src_res], [otr])
                fw.op("act", lambda: nc.scalar.activation(out=dt[dh:dh + 64, 0:width], in_=dt[dh:dh + 64, 0:width], func=AF.Exp, scale=-1.0),
                      [dtr], [dtr])

                def stage2():
                    fw.op("pe", lambda: nc.tensor.matmul(banks[6][oh:oh + 64, 0:width], lhsT=identF[dh:dh + 64, dh:dh + 64],
                                                          rhs=dt[dh:dh + 64, 0:width], start=True, stop=True),
                          [dtr, R["identF"]], [BK[6]])
                    fw.op("act", lambda: nc.scalar.copy(out=rd[oh:oh + 64, 0:width], in_=banks[6][oh:oh + 64, 0:width]),
                          [BK[6]], [rdr])
                    o = ctr["ost"] % 4
                    ctr["ost"] += 1
                    fw.op("dve", lambda: nc.vector.tensor_tensor(out=ost[o][oh:oh + 64, 0:width], in0=ot[oh:oh + 64, 0:width],
                                                                  in1=rd[oh:oh + 64, 0:width], op=ALU.mult),
                          [otr, rdr], [R["ost%d" % o]])
                    fw.dma("pool", dst_dram, ost[o][oh:oh + 64, 0:width], reads=[R["ost%d" % o]], writes=[R["Odram"]])
                pipe.defer(10, stage2)

            def make_item(Kb, Kres, Qb, Qres, half, chunks, ob, d, bias_list, Vt, Vres, on_done):
                nchunks = len(chunks)
                pts = []
                for ci, ch in enumerate(chunks):
                    pts.append(ctr["pt"] % 8)
                    ctr["pt"] += 1
                width = max(ch["qoff"] + ch["n"] for ch in chunks)
                blocks = []
                for off in range(0, width, 128):
                    cov = [(ci, off - ch["qoff"]) for ci, ch in enumerate(chunks) if ch["qoff"] <= off < ch["qoff"] + ch["n"]]
                    blocks.append((off, cov))
                ready = {}
                for bi_, (off, cov) in enumerate(blocks):
                    ready.setdefault(max(ci for ci, _ in cov), []).append(bi_)
                for ci, ch in enumerate(chunks):
                    st_i = ctr["st"] % 3
                    ctr["st"] += 1
                    pt_i = pts[ci]
                    n = ch["n"]

                    def qk(ch=ch, st_i=st_i, pt_i=pt_i, n=n):
                        k0 = ch["kcol0"]
                        q0 = ch["qcol0"]
                        lhsT = Kb[:, half, k0:k0 + 127 * d + 1:d]
                        rhs = Qb[:, q0:q0 + (n - 1) * d + 1:d]
                        nb_ = len(bias_list)
                        mm(banks[st_i][:, 0:n], lhsT, rhs, True, nb_ == 0, [Kres, Qres], [BK[st_i]])
                        for bi, (btile, bres, boff) in enumerate(bias_list):
                            mm(banks[st_i][:, 0:n], identB[:], btile[:, boff + ch["bias_c0"]:boff + ch["bias_c0"] + n],
                               False, bi == nb_ - 1, [R["identB"], bres], [BK[st_i]])
                        mk = ch["mask"]
                        if mk is None:
                            fw.op("act", lambda: nc.scalar.activation(out=PT[:, pt_i, 0:n], in_=banks[st_i][:, 0:n], func=AF.Exp,
                                                                       scale=0.125),
                                  [BK[st_i]], [R["PT%d" % pt_i]])
                        else:
                            fw.op("act", lambda: nc.scalar.activation(out=PT[:, pt_i, 0:n], in_=banks[st_i][:, 0:n], func=AF.Exp,
                                                                       bias=maskt[:, mk:mk + 1], scale=0.125),
                                  [BK[st_i], R["maskt"]], [R["PT%d" % pt_i]])

                    def pv(ci=ci):
                        for bi_ in ready.get(ci, []):
                            off, cov = blocks[bi_]
                            for k_, (cj, pc) in enumerate(cov):
                                mm(banks[3 + ob][:, off:off + 128], Vt[:, chunks[cj]["vtile"], :], PT[:, pts[cj], pc:pc + 128],
                                   k_ == 0, k_ == len(cov) - 1, [Vres, R["PT%d" % pts[cj]]], [BK[3 + ob]])
                        if ci == nchunks - 1:
                            on_done()
                    pipe.add(qk, pv)

            for si in range(3):
                s0 = SEG_START[si]
                c0 = si * SEGL
                mcol = si * 4
                for g in range(2):
                    kb_i = ctr["k"] % 2
                    ctr["k"] += 1
                    Kb, Kres = Kbuf[kb_i], R["Kbuf%d" % kb_i]
                    for e_ in range(2):
                        fw.dma("sp", Kb[e_ * 64:e_ * 64 + 64, e_, 0:4352], QK0s[4 + g, e_ * 64:e_ * 64 + 64, s0 - 128:s0 + 4096 + 128],
                               writes=[Kres])
                    vb_i = ctr["v"] % 2
                    ctr["v"] += 1
                    Vb = Vbuf[vb_i]
                    for par in range(2):
                        for t0 in range(0, 34, 17):
                            src = V0s[s0 - 128 + t0 * 128:s0 - 128 + (t0 + 17) * 128, g * 64:g * 64 + 64]
                            fw.dma("sp", Vb[:, par, t0:t0 + 17, par * 64:par * 64 + 64],
                                   src.rearrange("(c k) f -> k c f", k=128), writes=[R["Vbuf%d_%d" % (vb_i, par)]])
                    for qc in (2 * g, 2 * g + 1):
                        qb_i = ctr["q"] % 3
                        ctr["q"] += 1
                        Qb, Qres = Qbuf[qb_i], R["Qbuf%d" % qb_i]
                        fw.dma("sp", Qb[:], QK0s[qc, :, s0:s0 + 4096], writes=[Qres])
                        for e in range(2):
                            h = 2 * qc + e
                            for m in range(8):
                                ob = ctr["ob"] % 3
                                ctr["ob"] += 1
                                chunks = []
                                for c in range(4 * m - 1, 4 * m + 5):
                                    blocks = [b for b in (c - 1, c, c + 1) if 4 * m <= b < 4 * m + 4]
                                    if not blocks:
                                        continue
                                    b0 = blocks[0]
                                    mk = None
                                    if c == -1:
                                        mk = mcol + 2
                                    elif c == 32:
                                        mk = mcol + 3
                                    chunks.append(dict(kcol0=128 + 128 * c, qcol0=128 * b0, n=128 * len(blocks),
                                                       qoff=128 * (b0 - 4 * m), bias_c0=128 * (b0 - (c - 1)), mask=mk,
                                                       vtile=c + 1))
                                dst = O0s[qc, e * 64:e * 64 + 64, c0 + m * 512:c0 + (m + 1) * 512]

                                def done(ob=ob, e=e, h=h, dst=dst):
                                    finalize(banks[3 + ob], BK[3 + ob], 512, e, es[:, h:h + 1], dst)
                                Vt = Vb[:, e, :, :]
                                make_item(Kb, Kres, Qb, Qres, e, chunks, ob, 1, [(biasA, R["biasA"], h * 384)], Vt,
                                          R["Vbuf%d_%d" % (vb_i, e)], done)
                        pipe.run()
                for hp in range(2):
                    for i3, d in enumerate((1, 4, 16)):
                        Ls = SEGL // d
                        Hd = 64 * d
                        nchk = Ls // 128 + 1
                        kb_i = ctr["k"] % 2
                        ctr["k"] += 1
                        Kb, Kres = Kbuf[kb_i], R["Kbuf%d" % kb_i]
                        for e_ in range(2):
                            fw.dma("sp", Kb[e_ * 64:e_ * 64 + 64, e_, 0:4096 + 2 * Hd],
                                   QK0s[6 + 4 * i3 + 2 + hp, e_ * 64:e_ * 64 + 64, s0 - Hd:s0 + 4096 + Hd], writes=[Kres])
                        qb_i = ctr["q"] % 3
                        ctr["q"] += 1
                        Qb, Qres = Qbuf[qb_i], R["Qbuf%d" % qb_i]
                        fw.dma("sp", Qb[:], QK0s[6 + 4 * i3 + hp, :, s0:s0 + 4096], writes=[Qres])
                        vb_i = ctr["v"] % 2
                        ctr["v"] += 1
                        Vb = Vbuf[vb_i]
                        for e in range(2):
                            hB = 2 * hp + e
                            vcol = 128 + 256 * i3 + 64 * hB
                            a00 = s0 - Hd
                            if d == 16:
                                for c in range(nchk):
                                    src = V0s[a00 + d * 128 * c:a00 + d * 128 * (c + 1), vcol:vcol + 64].rearrange("(k r) f -> k r f", r=d)
                                    dst = Vb[:, e, c:c + (d - 1) * nchk + 1:nchk, e * 64:e * 64 + 64]
                                    fw.dma("sp", dst, src, writes=[R["Vbuf%d_%d" % (vb_i, e)]])
                            else:
                                for r in range(d):
                                    a0 = a00 + r
                                    tstep = 11 if nchk > 11 else nchk
                                    for t0 in range(0, nchk, tstep):
                                        tn = min(tstep, nchk - t0)
                                        src = V0s[a0 + d * 128 * t0:a0 + d * (128 * (t0 + tn) - 1) + 1:d, vcol:vcol + 64]
                                        fw.dma("sp", Vb[:, e, r * nchk + t0:r * nchk + t0 + tn, e * 64:e * 64 + 64],
                                               src.rearrange("(c k) f -> k c f", k=128),
                                               writes=[R["Vbuf%d_%d" % (vb_i, e)]])
                        for e in range(2):
                            hB = 2 * hp + e
                            hidx = 4 * i3 + hB
                            BW = min(512, Ls)
                            nbk = Ls // BW
                            bpb = BW // 128
                            for r in range(d):
                                for m in range(nbk):
                                    ob = ctr["ob"] % 3
                                    ctr["ob"] += 1
                                    chunks = []
                                    for c in range(bpb * m, bpb * m + bpb + 1):
                                        blocks = [b for b in (c - 1, c) if bpb * m <= b < bpb * m + bpb]
                                        if not blocks:
                                            continue
                                        b0 = blocks[0]
                                        mk = None
                                        if c == 0:
                                            mk = mcol + 0
                                        elif c == nchk - 1:
                                            mk = mcol + 1
                                        chunks.append(dict(kcol0=r + d * (128 * c), qcol0=r + d * 128 * b0, n=128 * len(blocks),
                                                           qoff=128 * (b0 - bpb * m), bias_c0=128 * (b0 - (c - 1)), mask=mk,
                                                           vtile=r * nchk + c))

                                    def done(ob=ob, e=e, r=r, m=m, d=d, BW=BW, i3=i3):
                                        a = acc[e]
                                        ar = R["acc%d" % e]
                                        dstv = a[:, r + d * m * BW:r + d * (m * BW + BW - 1) + 1:d]
                                        if i3 == 0:
                                            fw.op("dve", lambda: nc.vector.tensor_copy(out=dstv, in_=banks[3 + ob][:, 0:BW]),
                                                  [BK[3 + ob]], [ar])
                                        else:
                                            fw.op("dve", lambda: nc.vector.tensor_tensor(out=dstv, in0=dstv, in1=banks[3 + ob][:, 0:BW],
                                                                                          op=ALU.add),
                                                  [BK[3 + ob], ar], [ar])
                                    Vt = Vb[:, e, :, :]
                                    make_item(Kb, Kres, Qb, Qres, e, chunks, ob, d,
                                              [(biasB, R["biasB"], hidx * 512), (biasB, R["biasB"], hidx * 512 + 256)], Vt,
                                              R["Vbuf%d_%d" % (vb_i, e)], done)
                        pipe.run()
                    for e in range(2):
                        for m in range(8):
                            dst = O0s[4 + hp, e * 64:e * 64 + 64, c0 + m * 512:c0 + (m + 1) * 512]
                            finalize(acc[e][:, m * 512:(m + 1) * 512], R["acc%d" % e], 512, e, None, dst)
                            pipe.flush()
            pipe.flush()
        fw.barrier()

        def w13_load(l, j, w13, pfx):
            fw.dma("sp", w13[j % 3][:], W13s[l, j], writes=[R[pfx + "w13_%d" % (j % 3)]])

        def ffn_block(ph, l, xt4, xres4, xnT, xnres, W2r, gT, w13, pfx, hook=None, down_banks=3, after_up=None):
            for j in range(NJ):
                wi = j % 3
                wres = R[pfx + "w13_%d" % wi]
                b1, b3 = (0, 1) if j % 2 == 0 else (2, 3)
                for kc in range(8):
                    mm(banks[b1][:, :], w13[wi][:, kc, 0:128], xnT[:, kc, :], kc == 0, kc == 7, [wres, xnres], [BK[b1]])
                for kc in range(8):
                    mm(banks[b3][:, :], w13[wi][:, kc, 128:256], xnT[:, kc, :], kc == 0, kc == 7, [wres, xnres], [BK[b3]])
                if j + 3 < NJ:
                    w13_load(l, j + 3, w13, pfx)
                sl = silu_t[j % 2]
                slr = R[pfx + "silu%d" % (j % 2)]
                fw.op("act", lambda: nc.scalar.activation(out=sl[:], in_=banks[b1][:, :], func=AF.Silu), [BK[b1]], [slr])
                fw.op("dve", lambda: nc.vector.tensor_tensor(out=gT[:, j, :], in0=sl[:], in1=banks[b3][:, :], op=ALU.mult),
                      [slr, BK[b3]], [R[pfx + "gT"]])
                if hook is not None:
                    hook(j)
            if after_up is not None:
                after_up()
            nb = 0
            for t in range(4):
                for hf in range(2):
                    bk = 4 + (nb % down_banks)
                    for j in range(NJ):
                        mm(banks[bk][:, :], gT[:, j, t * 128:(t + 1) * 128], W2r[:, j, hf * 512:(hf + 1) * 512], j == 0, j == NJ - 1,
                           [R[pfx + "gT"], R[pfx + "W2r"]], [BK[bk]])
                    fw.op("dve", lambda: nc.vector.tensor_tensor(out=xt4[t][:, hf * 512:(hf + 1) * 512],
                                                                  in0=xt4[t][:, hf * 512:(hf + 1) * 512],
                                                                  in1=banks[bk][:, :], op=ALU.add),
                          [BK[bk], xres4[t]], [xres4[t]])
                    if hook is not None:
                        hook(NJ + nb)
                    nb += 1

        def load_W2r(ph, l, W2r, stg, stgres, pfx):
            for j in range(NJ):
                fw.dma("sp", stg[:, 0:D], ffn_w2[l, j * 128:(j + 1) * 128, :], writes=[stgres])
                e = "dve" if j % 2 == 0 else "pool"
                engh = nc.vector if j % 2 == 0 else nc.gpsimd
                fw.op(e, lambda: engh.tensor_copy(out=W2r[:, j, :], in_=stg[:, 0:D]), [stgres], [R[pfx + "W2r"]])

        def outproj_residual(oT, ores, nkc, Wo, wres, xt4, xres4, tiles=(0, 1, 2, 3)):
            nb = 0
            for t in tiles:
                for hf in range(2):
                    bk = nb % 2
                    nb += 1
                    for kc in range(nkc):
                        mm(banks[bk][:, :], oT[:, kc, t * 128:(t + 1) * 128], Wo[:, kc, hf * 512:(hf + 1) * 512], kc == 0,
                           kc == nkc - 1, [ores, wres], [BK[bk]])
                    fw.op("dve", lambda: nc.vector.tensor_tensor(out=xt4[t][:, hf * 512:(hf + 1) * 512],
                                                                  in0=xt4[t][:, hf * 512:(hf + 1) * 512],
                                                                  in1=banks[bk][:, :], op=ALU.add),
                          [BK[bk], xres4[t]], [xres4[t]])

        with contextlib.ExitStack() as ph:
            pfx = "p3"
            stg = sbuf(ph, "p3stg", [128, 1536], F32)
            stgres = R["p3stg"]
            W2r = sbuf(ph, "p3W2r", [128, NJ, D], BF16)
            load_W2r(ph, 0, W2r, stg, stgres, pfx)
            Wo = sbuf(ph, "p3Wo", [128, 6, D], BF16)
            for kc in range(6):
                load_cast(stg, stgres, w_out_ab[kc * 128:(kc + 1) * 128, :], D, [(0, Wo[:, kc, :], D, R["p3Wo"])])
            Wc = sbuf(ph, "p3Wc", [128, 8, 1536], BF16)
            for kc in range(8):
                load_cast(stg, stgres, w_in_c[kc * 128:(kc + 1) * 128, :], 1536, [(0, Wc[:, kc, :], 1536, R["p3Wc"])],
                          gain_ap=gains[:, 1, kc:kc + 1])
            gainqk = sbuf(ph, "gainqk", [128, 128], F32)
            fw.dma("sp", gainqk[:, 0:64], q_gain.partition_broadcast(128), writes=[R["gainqk"]])
            fw.dma("sp", gainqk[:, 64:128], k_gain.partition_broadcast(128), writes=[R["gainqk"]])
            w13 = [sbuf(ph, "p3w13_%d" % i, [128, 8, 256], BF16) for i in range(3)]
            silu_t = [sbuf(ph, "p3silu%d" % i, [128, 512], F32) for i in range(2)]
            gT = sbuf(ph, "p3gT", [128, NJ, 512], BF16)
            oT = [sbuf(ph, "p3oT%d" % i, [128, 6, 512], BF16) for i in range(1)]
            xtg3 = sbuf(ph, "p3xtg", [128, 4, D], F32)
            junk = sbuf(ph, "p3junk", [128, D], BF16)
            xs4 = [sbuf(ph, "p3xs4_%d" % i, [128, D], BF16) for i in range(4)]
            ssq = [sbuf(ph, "p3ss%d" % i, [128, 8], F32) for i in range(2)]
            xnT = [sbuf(ph, "p3xnT%d" % i, [128, 8, 512], BF16) for i in range(1)]
            qn = [sbuf(ph, "p3qn%d" % i, [128, 1280], F32) for i in range(2)]
            ssh = [sbuf(ph, "p3ssh%d" % i, [128, 20], F32) for i in range(2)]
            cs = [sbuf(ph, "p3cs%d" % i, [128, 64], F32) for i in range(2)]
            rt = [sbuf(ph, "p3rt%d" % i, [128, 640], F32) for i in range(4)]
            qr = [sbuf(ph, "p3qr%d" % i, [128, 1280], BF16) for i in range(2)]
            qTs = [sbuf(ph, "p3qTs%d" % i, [128, 10, 128], BF16) for i in range(2)]
            vs1 = [sbuf(ph, "p3vs%d" % i, [128, 256], BF16) for i in range(2)]
            xn1T = sbuf(ph, "p3xn1T", [128, 8, 512], BF16)
            xn1r = R["p3xn1T"]
            BQ = (4, 5, 6)
            tcnt = {"n": 0}

            def make_B(gi):
                si = gi // 8
                cbase = gi * 512
                own = si in OWN
                obase = OWN[si] + (gi % 8) * 512 if own else None
                tiles = []
                for t in range(4):
                    tc = tcnt["n"]
                    tcnt["n"] += 1
                    p2 = tc % 2
                    ctok = cbase + t * 128
                    blist = [0, 1, 2] if own else [2]
                    h0 = 0 if own else 16
                    nh = 20 - h0
                    qn_t, qn_r = qn[p2], R["p3qn%d" % p2]
                    ssh_t, ssh_r = ssh[p2], R["p3ssh%d" % p2]
                    v_t, v_r = vs1[p2], R["p3vs%d" % p2]
                    cs_t, cs_r = cs[p2], R["p3cs%d" % p2]
                    qr_t, qr_r = qr[p2], R["p3qr%d" % p2]
                    qT_t, qT_r = qTs[p2], R["p3qTs%d" % p2]

                    def st0(t=t, blist=blist, qn_t=qn_t, qn_r=qn_r, cs_t=cs_t, cs_r=cs_r, ctok=ctok):
                        for b in blist:
                            for kc in range(8):
                                mm(banks[BQ[b]][:, :], xn1T[:, kc, t * 128:(t + 1) * 128], Wc[:, kc, b * 512:(b + 1) * 512], kc == 0, kc == 7,
                                   [xn1r, R["p3Wc"]], [BK[BQ[b]]])
                        for b in blist:
                            w = 512 if b < 2 else 256
                            fw.op("act", lambda: nc.scalar.activation(out=qn_t[:, b * 512:b * 512 + w], in_=banks[BQ[b]][:, 0:w],
                                                                       func=AF.Square),
                                  [BK[BQ[b]]], [qn_r])
                        fw.dma("sp", cs_t[:, 0:32], cos_d[ctok:ctok + 128, :], writes=[cs_r])
                        fw.dma("sp", cs_t[:, 32:64], sin_d[ctok:ctok + 128, :], writes=[cs_r])

                    def st1(h0=h0, qn_t=qn_t, qn_r=qn_r, ssh_t=ssh_t, ssh_r=ssh_r, v_t=v_t, v_r=v_r, ctok=ctok):
                        fw.op("dve", lambda: nc.vector.tensor_reduce(out=ssh_t[:, h0:20],
                                                                     in_=qn_t[:, h0 * 64:1280].rearrange("p (h d) -> p h d", d=64),
                                                                     op=ALU.add, axis=AX.X),
                              [qn_r], [ssh_r])
                        fw.op("act", lambda: nc.scalar.activation(out=ssh_t[:, h0:20], in_=ssh_t[:, h0:20], func=AF.Sqrt,
                                                                   bias=epst[:, 0:1], scale=1.0 / 64),
                              [ssh_r, R["epst"]], [ssh_r])
                        fw.op("act", lambda: nc.scalar.copy(out=v_t[:], in_=banks[BQ[2]][:, 256:512]), [BK[BQ[2]]], [v_r])
                        if ctok < 4096:
                            for g_ in range(4):
                                fw.dma("pool", V1samp[g_][ctok:ctok + 128, :], v_t[:, g_ * 64:(g_ + 1) * 64], reads=[v_r], writes=[R["V1s"]])
                        else:
                            fw.dma("pool", V1p[ctok - 4096:ctok - 4096 + 128, :], v_t[:], reads=[v_r], writes=[R["V1s"]])

                    def st2(h0=h0, blist=blist, qn_t=qn_t, qn_r=qn_r, ssh_t=ssh_t, ssh_r=ssh_r):
                        fw.op("dve", lambda: nc.vector.reciprocal(out=ssh_t[:, h0:20], in_=ssh_t[:, h0:20]), [ssh_r], [ssh_r])
                        for b in blist:
                            w = 512 if b < 2 else 256
                            nhb = w // 64
                            fw.op("dve", lambda: nc.vector.tensor_tensor(
                                out=qn_t[:, b * 512:b * 512 + w].rearrange("p (h d) -> p h d", d=64),
                                in0=banks[BQ[b]][:, 0:w].rearrange("p (h d) -> p h d", d=64),
                                in1=ssh_t[:, b * 8:b * 8 + nhb].unsqueeze(2).to_broadcast([128, nhb, 64]), op=ALU.mult),
                                [BK[BQ[b]], ssh_r, qn_r], [qn_r])
                        if h0 == 0:
                            fw.op("pool", lambda: nc.gpsimd.tensor_tensor(
                                out=qn_t[:, 0:1024].rearrange("p (h d) -> p h d", d=64),
                                in0=qn_t[:, 0:1024].rearrange("p (h d) -> p h d", d=64),
                                in1=gainqk[:, 0:64].unsqueeze(1).to_broadcast([128, 16, 64]), op=ALU.mult),
                                [qn_r, R["gainqk"]], [qn_r])
                        fw.op("pool", lambda: nc.gpsimd.tensor_tensor(
                            out=qn_t[:, 1024:1280].rearrange("p (h d) -> p h d", d=64),
                            in0=qn_t[:, 1024:1280].rearrange("p (h d) -> p h d", d=64),
                            in1=gainqk[:, 64:128].unsqueeze(1).to_broadcast([128, 4, 64]), op=ALU.mult),
                            [qn_r, R["gainqk"]], [qn_r])

                    def st3(h0=h0, nh=nh, qn_t=qn_t, qn_r=qn_r, cs_t=cs_t, cs_r=cs_r):
                        qv = qn_t[:, h0 * 64:1280].rearrange("p (h i two) -> p h i two", i=32, two=2)
                        x0, x1 = qv[:, :, :, 0], qv[:, :, :, 1]
                        cb = cs_t[:, 0:32].unsqueeze(1).to_broadcast([128, nh, 32])
                        sb_ = cs_t[:, 32:64].unsqueeze(1).to_broadcast([128, nh, 32])
                        r4 = [rt[i][:, 0:nh * 32].rearrange("p (h i) -> p h i", i=32) for i in range(4)]
                        rr4 = [R["p3rt%d" % i] for i in range(4)]
                        fw.op("pool", lambda: nc.gpsimd.tensor_tensor(out=r4[1], in0=x1, in1=sb_, op=ALU.mult), [qn_r, cs_r], [rr4[1]])
                        fw.op("pool", lambda: nc.gpsimd.tensor_tensor(out=r4[3], in0=x1, in1=cb, op=ALU.mult), [qn_r, cs_r], [rr4[3]])
                        fw.op("dve", lambda: nc.vector.tensor_tensor(out=r4[0], in0=x0, in1=cb, op=ALU.mult), [qn_r, cs_r], [rr4[0]])
                        fw.op("dve", lambda: nc.vector.tensor_tensor(out=r4[2], in0=x0, in1=sb_, op=ALU.mult), [qn_r, cs_r], [rr4[2]])

                    def st3b(h0=h0, nh=nh, qr_t=qr_t, qr_r=qr_r):
                        r4 = [rt[i][:, 0:nh * 32].rearrange("p (h i) -> p h i", i=32) for i in range(4)]
                        rr4 = [R["p3rt%d" % i] for i in range(4)]
                        qrv = qr_t[:, h0 * 64:1280].rearrange("p (h i two) -> p h i two", i=32, two=2)
                        fw.op("dve", lambda: nc.vector.tensor_tensor(out=qrv[:, :, :, 0], in0=r4[0], in1=r4[1], op=ALU.subtract),
                              [rr4[0], rr4[1]], [qr_r])
                        fw.op("dve", lambda: nc.vector.tensor_tensor(out=qrv[:, :, :, 1], in0=r4[2], in1=r4[3], op=ALU.add),
                              [rr4[2], rr4[3]], [qr_r])

                    def st4(t=t, own=own, obase=obase, ctok=ctok, qr_t=qr_t, qr_r=qr_r, qT_t=qT_t, qT_r=qT_r):
                        if own:
                            for c in range(8):
                                fw.op("pe", lambda: nc.tensor.transpose(out=pT[:, c * 128:(c + 1) * 128], in_=qr_t[:, c * 128:(c + 1) * 128],
                                                                         identity=identB[:]),
                                      [qr_r, R["identB"]], [RpT], inc=(c == 7))
                            fw.op("act", lambda: nc.scalar.copy(out=qT_t[:, 0:8, :], in_=pT[:].rearrange("p (k t) -> p k t", k=8)),
                                  [RpT], [qT_r])
                            otok = obase + t * 128
                            fw.dma("pool", Q1s[:, :, otok:otok + 128].rearrange("c p t -> p c t"), qT_t[:, 0:8, :], reads=[qT_r],
                                   writes=[R["Q1s"]])
                        for c in range(8, 10):
                            fw.op("pe", lambda: nc.tensor.transpose(out=pT[:, (c - 8) * 128:(c - 7) * 128], in_=qr_t[:, c * 128:(c + 1) * 128],
                                                                     identity=identB[:]),
                                  [qr_r, R["identB"]], [RpT], inc=(c == 9))
                        fw.op("act", lambda: nc.scalar.copy(out=qT_t[:, 8:10, :], in_=pT[:, 0:256].rearrange("p (k t) -> p k t", k=2)),
                              [RpT], [qT_r])
                        for kv in range(4):
                            srcp = qT_t[(kv % 2) * 64:(kv % 2) * 64 + 64, 8 + kv // 2, :]
                            if ctok < 4096:
                                kdst = K1samp[kv][:, ctok:ctok + 128]
                            else:
                                kdst = K1p[kv, :, ctok - 4096:ctok - 4096 + 128]
                            fw.dma("pool", kdst, srcp, reads=[qT_r], writes=[R["K1s"]])
                    tiles.append([st0, st1, st2, st3, st3b, st4])
                return tiles

            def make_hook(Bt):
                sched = {}
                offs = (0, 1, 2, 3, 5, 9)
                for t in range(4):
                    for sidx in range(6):
                        sched.setdefault(5 * t + offs[sidx], []).append(Bt[t][sidx])

                def hook(j):
                    for fn in sched.get(j, []):
                        fn()
                return hook

            NG3 = NC_TOK // 512
            prevB = None
            for j_ in range(3):
                w13_load(0, j_, w13, pfx)
            for gi in range(NG3):
                si = gi // 8
                a0 = SEG_START[si] + (gi % 8) * 512
                cbase = gi * 512
                own = si in OWN
                obase = OWN[si] + (gi % 8) * 512 if own else None
                o_t = oT[0]
                o_r = R["p3oT0"]
                if gi == 0:
                    fw.dma("sp", o_t[:, :, :], O0s[:, :, cbase:cbase + 512].rearrange("c p t -> p c t"), writes=[o_r])
                xt4 = [xtg3[:, t, :] for t in range(4)]
                xres4 = [R["p3x%d" % t] for t in range(4)]
                fw.dma("sp", xtg3[:, :, :], xin[a0:a0 + 512, :].rearrange("(t p) f -> p t f", p=128), writes=xres4)
                xn = xnT[0]
                xnr = R["p3xnT0"]
                sq = ssq[gi % 2]
                sqr = R["p3ss%d" % (gi % 2)]
                for t in range(4):
                    outproj_residual(o_t, o_r, 6, Wo, R["p3Wo"], xt4, xres4, tiles=(t,))
                    sumsq(xt4[t][:], xres4[t], junk[:], R["p3junk"], sq[:, t:t + 1], sqr)
                    rstd_from_ss(sq[:, t:t + 1], sqr, sq[:, t:t + 1], sqr, D)
                    scale_only(xt4[t][:], xres4[t], sq[:, t:t + 1], sqr, xs4[t], R["p3xs4_%d" % t])
                for t in range(4):
                    transpose_only(xs4[t], R["p3xs4_%d" % t], xn, xnr, t * 128)
                post = {}

                def ap1(t, xt4=xt4, xres4=xres4, sq=sq, sqr=sqr, obase=obase):
                    fw.dma("pool", X1s[obase + t * 128:obase + (t + 1) * 128, :], xt4[t][:], reads=[xres4[t]], writes=[R["X1s"]])
                    sumsq(xt4[t][:], xres4[t], junk[:], R["p3junk"], sq[:, 4 + t:5 + t], sqr)
                    rstd_from_ss(sq[:, 4 + t:5 + t], sqr, sq[:, 4 + t:5 + t], sqr, D)
                    scale_only(xt4[t][:], xres4[t], sq[:, 4 + t:5 + t], sqr, xs4[t], R["p3xs4_%d" % t])

                def ap2(t):
                    transpose_only(xs4[t], R["p3xs4_%d" % t], xn1T, xn1r, t * 128)
                for t in range(4):
                    post.setdefault(NJ + 2 * t + 1, []).append(lambda t=t: ap1(t))
                    post.setdefault(NJ + 2 * t + 3, []).append(lambda t=t: ap2(t))
                bhook = make_hook(prevB) if prevB else None

                def hook(slot, bhook=bhook, post=post, gi=gi):
                    if slot == 8 and gi + 1 < NG3:
                        fw.dma("sp", oT[0][:, :, :], O0s[:, :, (gi + 1) * 512:(gi + 2) * 512].rearrange("c p t -> p c t"),
                               writes=[R["p3oT0"]])
                    if bhook is not None:
                        bhook(slot)
                    for fn in post.pop(slot, []):
                        fn()
                def after_up3(gi=gi):
                    if gi + 1 < NG3:
                        for j_ in range(3):
                            w13_load(0, j_, w13, pfx)
                ffn_block(ph, 0, xt4, xres4, xn, xnr, W2r, gT, w13, pfx, hook=hook, down_banks=2, after_up=after_up3)
                for slot in sorted(post):
                    for fn in post[slot]:
                        fn()
                prevB = make_B(gi)
            for t in range(4):
                for fn in prevB[t]:
                    fn()
        fw.barrier()
        CC_GROUPS = [[0, 1, 2, 3], [4, 5, 6, 7]]
        for g_ in range(4):
            fw.coll("AllGather", CC_GROUPS, K1samp[g_], K1g[g_], R["K1g%d" % g_])
            fw.coll("AllGather", CC_GROUPS, V1samp[g_], V1g[g_], R["V1g%d" % g_])

        with contextlib.ExitStack() as ph:
            K1b = sbuf(ph, "K1b", [128, 2, 16384], BF16)
            V1b = sbuf(ph, "V1b", [128, 2, 128, 128], BF16)
            Qb1 = [sbuf(ph, "Qb1_%d" % i, [128, 4096], BF16) for i in range(2)]
            PT = sbuf(ph, "PT4", [128, 6, 512], BF16)
            ost = [sbuf(ph, "ost4_%d" % i, [128, 512], BF16) for i in range(4)]
            Dt = [sbuf(ph, "Dt4_%d" % i, [128, 512], F32) for i in range(2)]
            RDs = [sbuf(ph, "RDs4_%d" % i, [128, 512], F32) for i in range(2)]
            Ot = [sbuf(ph, "Ot4_%d" % i, [128, 512], F32) for i in range(2)]
            for par in range(2):
                fw.op("pool", lambda: nc.gpsimd.memset(V1b[:, par, :, (1 - par) * 64:(1 - par) * 64 + 64], 1.0), [],
                      [R["V1b_%d" % par]])
            fw.op("pool", lambda: nc.gpsimd.memset(K1b[64:128, 0, :], 0.0), [], [R["K1b"]])
            fw.op("pool", lambda: nc.gpsimd.memset(K1b[0:64, 1, :], 0.0), [], [R["K1b"]])
            pipe = Pipe()
            ctr = {"st": 0, "ob": 0, "fin": 0, "ost": 0, "q": 0}
            STP = [PA, PB]
            STR = [R["PAres"], R["PBres"]]

            def finalize4(ob, par, dst_dram):
                oh, dh = (0, 64) if par == 0 else (64, 0)
                f = ctr["fin"] % 2
                ctr["fin"] += 1
                dt, rd, ot = Dt[f], RDs[f], Ot[f]
                dtr, rdr, otr = R["Dt4_%d" % f], R["RDs4_%d" % f], R["Ot4_%d" % f]
                src_ap, src_res = banks[4 + ob], BK[4 + ob]
                fw.op("dve", lambda: nc.vector.tensor_copy(out=dt[dh:dh + 64, :], in_=src_ap[dh:dh + 64, :]), [src_res], [dtr])
                fw.op("dve", lambda: nc.vector.tensor_copy(out=ot[oh:oh + 64, :], in_=src_ap[oh:oh + 64, :]), [src_res], [otr])
                fw.op("dve", lambda: nc.vector.reciprocal(out=dt[dh:dh + 64, :], in_=dt[dh:dh + 64, :]), [dtr], [dtr])

                def stage2():
                    fw.op("pe", lambda: nc.tensor.matmul(banks[6][oh:oh + 64, :], lhsT=identF[dh:dh + 64, dh:dh + 64],
                                                          rhs=dt[dh:dh + 64, :], start=True, stop=True),
                          [dtr, R["identF"]], [BK[6]])
                    fw.op("dve", lambda: nc.vector.tensor_copy(out=rd[oh:oh + 64, :], in_=banks[6][oh:oh + 64, :]), [BK[6]], [rdr])
                    o = ctr["ost"] % 4
                    ctr["ost"] += 1
                    fw.op("dve", lambda: nc.vector.tensor_tensor(out=ost[o][oh:oh + 64, :], in0=ot[oh:oh + 64, :],
                                                                  in1=rd[oh:oh + 64, :], op=ALU.mult),
                          [otr, rdr], [R["ost4_%d" % o]])
                    fw.dma("pool", dst_dram, ost[o][oh:oh + 64, :], reads=[R["ost4_%d" % o]], writes=[R["O1s"]])
                pipe.defer(10, stage2)

            seqs = [(4096, "p", 0, 32), (8192, "p", 4096, 32), (0, "s", 0, 128)]
            for (qbase, kind, kbase, nkt) in seqs:
                for g in range(4):
                    for e_ in range(2):
                        if kind == "p":
                            fw.dma("sp", K1b[e_ * 64:e_ * 64 + 64, e_, 0:4096], K1p[g, :, kbase:kbase + 4096],
                                   writes=[R["K1b"]])
                        else:
                            for r_ in range(4):
                                fw.dma("sp", K1b[e_ * 64:e_ * 64 + 64, e_, r_ * 4096:(r_ + 1) * 4096],
                                       K1g[g][r_ * 64:r_ * 64 + 64, :],
                                       reads=[R["K1g%d" % g]], writes=[R["K1b"]])
                    vres = [] if kind == "p" else [R["V1g%d" % g]]
                    for par in range(2):
                        for t0 in range(0, nkt, 16):
                            if kind == "p":
                                src = V1p[kbase + t0 * 128:kbase + (t0 + 16) * 128, g * 64:g * 64 + 64]
                            else:
                                src = V1g[g][kbase + t0 * 128:kbase + (t0 + 16) * 128, :]
                            fw.dma("sp", V1b[:, par, t0:t0 + 16, par * 64:par * 64 + 64],
                                   src.rearrange("(c k) f -> k c f", k=128), reads=vres, writes=[R["V1b_%d" % par]])
                    for qc in (2 * g, 2 * g + 1):
                        qb_i = ctr["q"] % 2
                        ctr["q"] += 1
                        Qb, Qres = Qb1[qb_i], R["Qb1_%d" % qb_i]
                        fw.dma("sp", Qb[:], Q1s[qc, :, qbase:qbase + 4096], writes=[Qres])
                        for e in range(2):
                            for m in range(8):
                                ob = ctr["ob"] % 2
                                ctr["ob"] += 1
                                dst = O1s[qc, e * 64:e * 64 + 64, qbase + m * 512:qbase + (m + 1) * 512]
                                for c2 in range(nkt // 2):
                                    st_i = ctr["st"] % 2
                                    pt_i = ctr["st"] % 3
                                    ctr["st"] += 1

                                    def qk(c2=c2, st_i=st_i, pt_i=pt_i, e=e, m=m, Qb=Qb, Qres=Qres):
                                        for u in range(2):
                                            c = 2 * c2 + u
                                            mm(STP[st_i][:, u * 512:(u + 1) * 512], K1b[:, e, c * 128:(c + 1) * 128],
                                               Qb[:, m * 512:(m + 1) * 512], True, True, [R["K1b"], Qres], [STR[st_i]])
                                        fw.op("act", lambda: nc.scalar.activation(
                                            out=PT[:, 2 * pt_i:2 * pt_i + 2, :].rearrange("p a b -> p (a b)"), in_=STP[st_i][:, :],
                                            func=AF.Exp, scale=0.125),
                                            [STR[st_i]], [R["PT4_%d" % pt_i]])

                                    def pv(c2=c2, pt_i=pt_i, e=e, ob=ob, dst=dst, nkt=nkt):
                                        for u in range(2):
                                            c = 2 * c2 + u
                                            mm(banks[4 + ob][:, :], V1b[:, e, c, :], PT[:, 2 * pt_i + u, :], c == 0, c == nkt - 1,
                                               [R["V1b_%d" % e], R["PT4_%d" % pt_i]], [BK[4 + ob]])
                                        if c2 == nkt // 2 - 1:
                                            finalize4(ob, e, dst)
                                    pipe.add(qk, pv)
                        pipe.run(look=2)
            pipe.flush()
        fw.barrier()

        with contextlib.ExitStack() as ph:
            pfx = "p5"
            stg = sbuf(ph, "p5stg", [128, D], F32)
            stgres = R["p5stg"]
            W2r = sbuf(ph, "p5W2r", [128, NJ, D], BF16)
            load_W2r(ph, 1, W2r, stg, stgres, pfx)
            Woc = sbuf(ph, "p5Woc", [128, 8, D], BF16)
            for kc in range(8):
                load_cast(stg, stgres, w_out_c[kc * 128:(kc + 1) * 128, :], D, [(0, Woc[:, kc, :], D, R["p5Woc"])])
            gfin = sbuf(ph, "gfin", [128, D], F32)
            fw.dma("sp", gfin[:], final_norm.partition_broadcast(128), writes=[R["gfin"]])
            w13 = [sbuf(ph, "p5w13_%d" % i, [128, 8, 256], BF16) for i in range(3)]
            silu_t = [sbuf(ph, "p5silu%d" % i, [128, 512], F32) for i in range(2)]
            gT = sbuf(ph, "p5gT", [128, NJ, 512], BF16)
            oT = [sbuf(ph, "p5oT%d" % i, [128, 8, 512], BF16) for i in range(2)]
            xtg5 = [sbuf(ph, "p5xtg%d" % i, [128, 4, D], F32) for i in range(2)]
            junk = sbuf(ph, "p5junk", [128, D], F32)
            xs4 = [sbuf(ph, "p5xs4_%d" % i, [128, D], BF16) for i in range(4)]
            ssq = [sbuf(ph, "p5ss%d" % i, [128, 8], F32) for i in range(2)]
            xnT = [sbuf(ph, "p5xnT%d" % i, [128, 8, 512], BF16) for i in range(1)]
            yt = [sbuf(ph, "p5y%d" % i, [128, D], F32) for i in range(2)]
            ny = 0
            NG5 = NOWN // 512

            def p5_bufs(gi):
                xt4 = [xtg5[gi % 2][:, t, :] for t in range(4)]
                xres4 = [R["p5x%d" % ((gi % 2) * 4 + t)] for t in range(4)]
                return oT[gi % 2], R["p5oT%d" % (gi % 2)], xt4, xres4, ssq[gi % 2], R["p5ss%d" % (gi % 2)]

            def p5_load(gi):
                o_t, o_r, xt4, xres4, sq, sqr = p5_bufs(gi)
                ob0 = gi * 512
                fw.dma("sp", o_t[:, :, :], O1s[:, :, ob0:ob0 + 512].rearrange("c p t -> p c t"), writes=[o_r])
                fw.dma("sp", xtg5[gi % 2][:, :, :], X1s[ob0:ob0 + 512, :].rearrange("(t p) f -> p t f", p=128), writes=xres4)

            def p5_ap_tile(gi, t):
                o_t, o_r, xt4, xres4, sq, sqr = p5_bufs(gi)
                outproj_residual(o_t, o_r, 8, Woc, R["p5Woc"], xt4, xres4, tiles=(t,))
                sumsq(xt4[t][:], xres4[t], junk[:], R["p5junk"], sq[:, t:t + 1], sqr)
                rstd_from_ss(sq[:, t:t + 1], sqr, sq[:, t:t + 1], sqr, D)
                scale_only(xt4[t][:], xres4[t], sq[:, t:t + 1], sqr, xs4[t], R["p5xs4_%d" % t])

            def p5_ap_tr(gi, t):
                transpose_only(xs4[t], R["p5xs4_%d" % t], xnT[0], R["p5xnT0"], t * 128)

            p5_load(0)
            for j_ in range(3):
                w13_load(1, j_, w13, pfx)
            for t in range(4):
                p5_ap_tile(0, t)
            for t in range(4):
                p5_ap_tr(0, t)
            for gi in range(NG5):
                ob0 = gi * 512
                o_t, o_r, xt4, xres4, sq, sqr = p5_bufs(gi)
                def after_up5(gi=gi):
                    if gi + 1 < NG5:
                        for j_ in range(3):
                            w13_load(1, j_, w13, pfx)

                def hook(slot, gi=gi):
                    if slot == 8 and gi + 1 < NG5:
                        p5_load(gi + 1)
                    if gi + 1 >= NG5 or slot < NJ:
                        return
                    k = slot - NJ
                    if k < 4:
                        p5_ap_tile(gi + 1, k)
                    else:
                        p5_ap_tr(gi + 1, k - 4)
                ffn_block(ph, 1, xt4, xres4, xnT[0], R["p5xnT0"], W2r, gT, w13, pfx, hook=hook, down_banks=2, after_up=after_up5)
                for t in range(4):
                    sumsq(xt4[t][:], xres4[t], junk[:], R["p5junk"], sq[:, 4 + t:5 + t], sqr)
                    rstd_from_ss(sq[:, 4 + t:5 + t], sqr, sq[:, 4 + t:5 + t], sqr, D)
                    y_t, y_r = yt[ny % 2], R["p5y%d" % (ny % 2)]
                    ny += 1
                    fw.op("dve", lambda: nc.vector.scalar_tensor_tensor(out=y_t[:], in0=xt4[t][:], scalar=sq[:, 4 + t:5 + t],
                                                                         in1=gfin[:], op0=ALU.mult, op1=ALU.mult),
                          [xres4[t], sqr, R["gfin"]], [y_r])
                    fw.dma("pool", yout[ob0 + t * 128:ob0 + (t + 1) * 128, :], y_t[:], reads=[y_r], writes=[R["yout"]])
        fw.barrier()
    return nc, fw


_CACHE = {}


def _tables():
    if "t" in _CACHE:
        return _CACHE["t"]
    import ml_dtypes
    k = np.arange(128)[:, None].astype(np.float64)
    slopesA = 2.0 ** (-8.0 * np.arange(1, 9) / 8)
    q = np.arange(384)[None, :].astype(np.float64)
    relA = np.abs(k - q + 128)
    biasA = np.zeros((128, 8 * 384), np.float32)
    for h in range(8):
        b = np.where(relA <= 128, -slopesA[h] * relA * 8.0, NEGM * 8.0)
        biasA[:, h * 384:(h + 1) * 384] = b
    slopesB = 2.0 ** (-8.0 * np.arange(1, 13) / 12)
    q = np.arange(256)[None, :].astype(np.float64)
    relB = np.abs(k - q + 64)
    biasB = np.zeros((128, 12 * 512), np.float32)
    dil = (1, 4, 16)
    for hidx in range(12):
        d = dil[hidx // 4]
        b = np.where(relB <= 64, -np.float32(slopesB[hidx]).astype(np.float64) * d * relB * 8.0, NEGM * 8.0)
        hi = b.astype(np.float32).astype(ml_dtypes.bfloat16).astype(np.float32)
        lo = (b - hi).astype(np.float32).astype(ml_dtypes.bfloat16).astype(np.float32)
        lo = np.where(relB <= 64, lo, 0.0)
        biasB[:, hidx * 512:hidx * 512 + 256] = hi
        biasB[:, hidx * 512 + 256:hidx * 512 + 512] = lo
    ident = np.eye(128, dtype=np.float32)
    _CACHE["t"] = (biasA, biasB, ident)
    return _CACHE["t"]


def _rope_tables(positions):
    n_freq = 16
    inv = (10000.0 ** (-np.arange(n_freq, dtype=np.float32) / n_freq)).astype(np.float32)
    row = (positions // 64).astype(np.float32)
    col = (positions % 64).astype(np.float32)
    ang = np.concatenate([row[:, None] * inv, col[:, None] * inv], axis=-1).astype(np.float32)
    return np.cos(ang).astype(np.float32), np.sin(ang).astype(np.float32)


def kernel(x_prompt, x_sample, norm_mix, w_in_ab, w_out_ab, sink_a, w_in_c, w_out_c, q_gain_c, k_gain_c,
           norm_ffn, ffn_w1, ffn_w3, ffn_w2, final_norm):
    f32 = lambda a: np.ascontiguousarray(np.asarray(a, dtype=np.float32))
    x_prompt, x_sample = f32(x_prompt), f32(x_sample)
    biasA, biasB, ident = _tables()
    if "nc" not in _CACHE:
        _CACHE["nc"] = build_program()
    nc, fw = _CACHE["nc"]
    shared = {
        "biasA": biasA, "biasB": biasB, "ident": ident,
        "norm_mix": f32(norm_mix), "w_in_ab": f32(w_in_ab)[0], "w_out_ab": f32(w_out_ab)[0], "sink_a": f32(sink_a),
        "w_in_c": f32(w_in_c)[0], "w_out_c": f32(w_out_c)[0], "q_gain_c": f32(q_gain_c), "k_gain_c": f32(k_gain_c),
        "norm_ffn": f32(norm_ffn), "ffn_w1": f32(ffn_w1), "ffn_w3": f32(ffn_w3), "ffn_w2": f32(ffn_w2),
        "final_norm": f32(final_norm).reshape(1, D),
    }
    in_maps = []
    for c in range(8):
        sq, qq = c // 4, c % 4
        xs = x_sample[sq]
        xin = np.zeros((NT0, D), np.float32)
        if qq > 0:
            xin[0:1024] = xs[qq * SEGL - 1024:qq * SEGL]
        xin[1024:5120] = xs[qq * SEGL:(qq + 1) * SEGL]
        if qq < 3:
            xin[5120:6144] = xs[(qq + 1) * SEGL:(qq + 1) * SEGL + 1024]
        xin[6144:10240] = x_prompt[2 * c]
        xin[10240:14336] = x_prompt[2 * c + 1]
        masks = np.zeros((128, 12), np.float32)
        for si in range(3):
            lv, rv = (qq != 0, qq != 3) if si == 0 else (False, False)
            if not lv:
                masks[0:64, si * 4 + 0] = NEGM
                masks[:, si * 4 + 2] = NEGM
            if not rv:
                masks[64:128, si * 4 + 1] = NEGM
                masks[:, si * 4 + 3] = NEGM
        pos = np.concatenate([np.arange(4096) + qq * SEGL, np.arange(4096), np.arange(4096)])
        cosT, sinT = _rope_tables(pos)
        m = dict(shared)
        m.update({"xin": xin, "masks": masks, "cosT": cosT, "sinT": sinT})
        in_maps.append(m)
    if _CACHE.get('debug_maps'):
        return in_maps
    res = run_bass_kernel_spmd(nc, in_maps, core_ids=list(range(8)))
    y_prompt = np.zeros((16, 4096, D), np.float32)
    y_sample = np.zeros((2, 16384, D), np.float32)
    for c in range(8):
        y = np.asarray(res.results[c]["yout"], dtype=np.float32)
        sq, qq = c // 4, c % 4
        y_sample[sq, qq * SEGL:(qq + 1) * SEGL] = y[0:4096]
        y_prompt[2 * c] = y[4096:8192]
        y_prompt[2 * c + 1] = y[8192:12288]
    return (y_prompt, y_sample)
```

```python
import contextlib
import numpy as np
import concourse.bass as bass
import concourse.mybir as mybir
from concourse.bass_utils import run_bass_kernel_spmd

F32 = mybir.dt.float32
BF16 = mybir.dt.bfloat16
AF = mybir.ActivationFunctionType
ALU = mybir.AluOpType
AX = mybir.AxisListType

D = 1024
DFF = 2816
NJ = 22
SEGL = 4096
NT0 = 15360
SEG_START = [1024, 6144, 10240]
OWN = {0: 0, 1: 4096, 2: 8192}
NC_TOK = 12288
NOWN = 12288
EPS = 1e-6
NEGM = -30000.0
DEBUG_OUT = ("O0s", "X1s", "Q1s", "O1s")


class Res:
    __slots__ = ("name", "w", "rd")

    def __init__(self, name=""):
        self.name = name
        self.w = None
        self.rd = {}


class FW:
    NDMA = 8

    def __init__(self, nc, stack):
        self.nc = nc
        self.engs = {"pe": nc.tensor, "act": nc.scalar, "dve": nc.vector, "pool": nc.gpsimd, "sp": nc.sync}
        self.sems = {}
        self.cnt = {}
        for e in self.engs:
            self.sems[e] = stack.enter_context(nc.semaphore("s_" + e))
            self.cnt[e] = 0
        self.dsem, self.dcnt, self.dnext = {}, {}, {}
        for q in ("sp", "pool", "act"):
            self.dsem[q] = [stack.enter_context(nc.semaphore(f"d_{q}{i}")) for i in range(self.NDMA)]
            self.dcnt[q] = [0] * self.NDMA
            self.dnext[q] = 0
        self.known = {e: {} for e in self.engs}
        self.ninstr = 0
        self.sems["cc"] = stack.enter_context(nc.semaphore("s_cc"))
        self.cnt["cc"] = 0

    def _sem(self, key):
        if isinstance(key, tuple):
            return self.dsem[key[0]][key[1]]
        return self.sems[key]

    def _wait(self, eng, key, val):
        k = self.known[eng]
        if k.get(key, 0) >= val:
            return
        self.engs[eng].wait_ge(self._sem(key), val)
        k[key] = val
        self.ninstr += 1

    def _deps(self, eng, reads, writes, selfsync=True):
        for r in reads:
            if r.w is not None:
                key, val, weng = r.w
                if weng != eng or (selfsync and eng != "pe"):
                    self._wait(eng, key, val)
        for w in writes:
            if w.w is not None:
                key, val, weng = w.w
                if weng != eng or (selfsync and eng != "pe"):
                    self._wait(eng, key, val)
            for reng, (key, val) in w.rd.items():
                if reng != eng or (selfsync and eng != "pe"):
                    self._wait(eng, key, val)

    def op(self, eng, fn, reads=(), writes=(), inc=True):
        self._deps(eng, reads, writes)
        ins = fn()
        self.ninstr += 1
        if inc:
            self.cnt[eng] += 1
            ins.then_inc(self.sems[eng], 1)
            val = self.cnt[eng]
        else:
            val = self.cnt[eng] + 1
        for r in reads:
            r.rd[eng] = (eng, val)
        for w in writes:
            w.w = (eng, val, eng)
            w.rd = {}
        return ins

    def dma(self, q, out, in_, reads=(), writes=(), **kw):
        i = self.dnext[q]
        self.dnext[q] = (i + 1) % self.NDMA
        if self.dcnt[q][i] > 0:
            self._wait(q, (q, i), self.dcnt[q][i])
        self._deps(q, reads, writes, selfsync=True)
        self.dcnt[q][i] += 16
        val = self.dcnt[q][i]
        ins = self.engs[q].dma_start(out=out, in_=in_, **kw)
        ins.then_inc(self.dsem[q][i], 16)
        self.ninstr += 1
        key = (q, i)
        for r in reads:
            r.rd["dma_%s%d" % (q, i)] = (key, val)
        for w in writes:
            w.w = (key, val, "dma")
            w.rd = {}
        return ins

    def coll(self, kind, groups, in_ap, out_ap, out_res):
        ins = self.nc.gpsimd.collective_compute(kind, mybir.AluOpType.bypass, replica_groups=groups, ins=[in_ap], outs=[out_ap])
        ins.then_inc(self.sems["cc"])
        self.cnt["cc"] += 1
        self.ninstr += 1
        out_res.w = ("cc", self.cnt["cc"], "cc")
        out_res.rd = {}

    def barrier(self):
        for e in self.engs:
            if self.cnt["cc"] > 0:
                self._wait(e, "cc", self.cnt["cc"])
            for e2 in self.engs:
                if e2 != e and self.cnt[e2] > 0:
                    self._wait(e, e2, self.cnt[e2])
            for q in self.dsem:
                for i in range(self.NDMA):
                    if self.dcnt[q][i] > 0:
                        self._wait(e, (q, i), self.dcnt[q][i])


class RD(dict):
    def __missing__(self, k):
        r = Res(str(k))
        self[k] = r
        return r


def build_program(debug=False):
    nc = bass.Bass("TRN2", target_bir_lowering=False)

    def din(name, shape, dt=F32):
        return nc.dram_tensor(name, list(shape), dt, kind="ExternalInput").ap()

    def dscr(name, shape, dt):
        kind = "ExternalOutput" if (debug and name in DEBUG_OUT) else "Internal"
        return nc.dram_tensor(name, list(shape), dt, kind=kind).ap()

    xin = din("xin", [NT0, D])
    masks = din("masks", [128, 12])
    biasA_d = din("biasA", [128, 8 * 384])
    biasB_d = din("biasB", [128, 12 * 512])
    cos_d = din("cosT", [NC_TOK, 32])
    sin_d = din("sinT", [NC_TOK, 32])
    ident_d = din("ident", [128, 128])
    norm_mix = din("norm_mix", [2, D])
    w_in_ab = din("w_in_ab", [D, 3072])
    w_out_ab = din("w_out_ab", [768, D])
    sink_a = din("sink_a", [1, 8])
    w_in_c = din("w_in_c", [D, 1536])
    w_out_c = din("w_out_c", [D, D])
    q_gain = din("q_gain_c", [1, 64])
    k_gain = din("k_gain_c", [1, 64])
    norm_ffn = din("norm_ffn", [2, D])
    ffn_w1 = din("ffn_w1", [2, D, DFF])
    ffn_w3 = din("ffn_w3", [2, D, DFF])
    ffn_w2 = din("ffn_w2", [2, DFF, D])
    final_norm = din("final_norm", [1, D])
    yout = nc.dram_tensor("yout", [NOWN, D], F32, kind="ExternalOutput").ap()

    W13s = dscr("W13s", [2, NJ, 128, 8, 256], BF16)
    QK0s = dscr("QK0s", [18, 128, NT0], BF16)
    V0s = dscr("V0s", [NT0, 896], BF16)
    O0s = dscr("O0s", [6, 128, NC_TOK], BF16)
    X1s = dscr("X1s", [NOWN, D], F32)
    Q1s = dscr("Q1s", [8, 128, NOWN], BF16)
    K1samp = [dscr("K1samp%d" % g, [64, 4096], BF16) for g in range(4)]
    V1samp = [dscr("V1samp%d" % g, [4096, 64], BF16) for g in range(4)]
    K1g = [nc.dram_tensor("K1g%d" % g, [256, 4096], BF16, kind="Internal", addr_space="Local").ap() for g in range(4)]
    V1g = [nc.dram_tensor("V1g%d" % g, [16384, 64], BF16, kind="Internal", addr_space="Local").ap() for g in range(4)]
    K1p = dscr("K1p", [4, 64, 8192], BF16)
    V1p = dscr("V1p", [8192, 256], BF16)
    O1s = dscr("O1s", [8, 128, NOWN], BF16)

    with contextlib.ExitStack() as top:
        fw = FW(nc, top)
        R = RD()
        PA = top.enter_context(nc.psum_tensor("PA", [128, 1024], F32))
        PB = top.enter_context(nc.psum_tensor("PB", [128, 1024], F32))
        banks = [PA[:, 0:512], PA[:, 512:1024], PB[:, 0:512], PB[:, 512:1024]]
        banks += [top.enter_context(nc.psum_tensor("bank%d" % i, [128, 512], F32))[:, :] for i in range(4, 7)]
        pT = top.enter_context(nc.psum_tensor("pT", [128, 1024], BF16))
        BK = [R["bank%d" % i] for i in range(7)]
        RpT = R["pT"]

        def sbuf(st, name, shape, dt):
            return st.enter_context(nc.sbuf_tensor(name, list(shape), dt))

        identF = sbuf(top, "identF", [128, 128], F32)
        identB = sbuf(top, "identB", [128, 128], BF16)
        maskt = sbuf(top, "maskt", [128, 12], F32)
        fw.dma("sp", identF[:], ident_d, writes=[R["identF"]])
        fw.dma("sp", maskt[:], masks, writes=[R["maskt"]])
        fw.op("dve", lambda: nc.vector.tensor_copy(out=identB[:], in_=identF[:]), [R["identF"]], [R["identB"]])

        rr = {"cp": 0}

        def copy_rr(out, in_, reads, writes, engines=("act", "dve")):
            e = engines[rr["cp"] % len(engines)]
            rr["cp"] += 1
            if e == "act":
                fw.op("act", lambda: nc.scalar.copy(out=out, in_=in_), reads, writes)
            elif e == "dve":
                fw.op("dve", lambda: nc.vector.tensor_copy(out=out, in_=in_), reads, writes)
            else:
                fw.op("pool", lambda: nc.gpsimd.tensor_copy(out=out, in_=in_), reads, writes)

        def mm(out, lhsT, rhs, start, stop, reads, writes):
            fw.op("pe", lambda: nc.tensor.matmul(out, lhsT=lhsT, rhs=rhs, start=start, stop=stop),
                  reads, writes, inc=stop)

        def load_cast(st_tile, st_res, src_ap, width, pieces, gain_ap=None, eng_rr=("dve", "pool")):
            fw.dma("sp", st_tile[:, 0:width], src_ap, writes=[st_res])
            for n, (sc, dst, w, dres) in enumerate(pieces):
                e = eng_rr[n % len(eng_rr)]
                if gain_ap is None:
                    if e == "dve":
                        fw.op("dve", lambda: nc.vector.tensor_copy(out=dst, in_=st_tile[:, sc:sc + w]), [st_res], [dres])
                    else:
                        fw.op("pool", lambda: nc.gpsimd.tensor_copy(out=dst, in_=st_tile[:, sc:sc + w]), [st_res], [dres])
                else:
                    engh = nc.vector if e == "dve" else nc.gpsimd
                    fw.op(e, lambda: engh.tensor_scalar(out=dst, in0=st_tile[:, sc:sc + w], scalar1=gain_ap,
                                                         scalar2=None, op0=ALU.mult),
                          [st_res, R["gains"]], [dres])

        gains = sbuf(top, "gains", [128, 4, 8], F32)
        for l in range(2):
            fw.dma("sp", gains[:, l, :], norm_mix[l:l + 1, :].rearrange("o (kc p) -> p (o kc)", p=128), writes=[R["gains"]], allow_slow_non_contiguous=True)
            fw.dma("sp", gains[:, 2 + l, :], norm_ffn[l:l + 1, :].rearrange("o (kc p) -> p (o kc)", p=128), writes=[R["gains"]], allow_slow_non_contiguous=True)
        with contextlib.ExitStack() as ph:
            stg = [sbuf(ph, "p0stg%d" % i, [128, DFF], F32) for i in range(2)]
            wst = [sbuf(ph, "p0w%d" % i, [128, NJ, 256], BF16) for i in range(2)]
            n = 0
            for l in range(2):
                for kc in range(8):
                    ws = wst[n % 2]
                    wres = R["p0w%d" % (n % 2)]
                    for wi, wsrc in enumerate((ffn_w1, ffn_w3)):
                        s = stg[wi]
                        sres = R["p0stg%d" % wi]
                        fw.dma("sp", s[:], wsrc[l, kc * 128:(kc + 1) * 128, :], writes=[sres])
                        e = "dve" if wi == 0 else "pool"
                        engh = nc.vector if wi == 0 else nc.gpsimd
                        fw.op(e, lambda: engh.tensor_scalar(
                            out=ws[:, :, wi * 128:(wi + 1) * 128],
                            in0=s[:].rearrange("p (j c) -> p j c", c=128),
                            scalar1=gains[:, 2 + l, kc:kc + 1], scalar2=None, op0=ALU.mult),
                            [sres, R["gains"]], [wres])
                    fw.dma("pool", W13s[l, :, :, kc, :].rearrange("j p c -> p j c"), ws[:], reads=[wres],
                           writes=[R["W13s"]])
                    n += 1
        fw.barrier()

        def norm_transpose(xt_ap, xres, ss_t, ssres, col, xs_t, xsres, xnT, xnres, tcol):
            raise NotImplementedError

        def sumsq(x_ap, xres, junk, junkres, ss_ap, ssres):
            fw.op("act", lambda: nc.scalar.activation(out=junk, in_=x_ap, func=AF.Square, accum_out=ss_ap),
                  [xres], [junkres, ssres])

        def rstd_from_ss(ss_ap, ssres, rs_ap, rsres, n):
            fw.op("act", lambda: nc.scalar.activation(out=rs_ap, in_=ss_ap, func=AF.Sqrt, bias=epst[:, 0:1], scale=1.0 / n),
                  [ssres, R["epst"]], [rsres])
            fw.op("dve", lambda: nc.vector.reciprocal(out=rs_ap, in_=rs_ap), [rsres], [rsres])

        epst = sbuf(top, "epst", [128, 1], F32)
        fw.op("pool", lambda: nc.gpsimd.memset(epst[:], EPS), [], [R["epst"]])

        def scale_only(x_ap, xres, rs_ap, rsres, xs_t, xsres):
            fw.op("dve", lambda: nc.vector.tensor_scalar(out=xs_t[:], in0=x_ap, scalar1=rs_ap, scalar2=None, op0=ALU.mult),
                  [xres, rsres], [xsres])

        def scale_transpose(x_ap, xres, rs_ap, rsres, xs_t, xsres, xnT, xnres, tok0):
            scale_only(x_ap, xres, rs_ap, rsres, xs_t, xsres)
            transpose_only(xs_t, xsres, xnT, xnres, tok0)

        pT2 = banks[6].bitcast(BF16)
        RpTh = [RpT, BK[6]]
        pTh = [pT, pT2]

        def transpose_only(xs_t, xsres, xnT, xnres, tok0):
            for half in range(2):
                for kc in range(4 * half, 4 * half + 4):
                    fw.op("pe", lambda: nc.tensor.transpose(out=pTh[half][:, (kc % 4) * 128:(kc % 4 + 1) * 128],
                                                             in_=xs_t[:, kc * 128:(kc + 1) * 128], identity=identB[:]),
                          [xsres, R["identB"]], [RpTh[half]], inc=(kc % 4 == 3))
                src = pTh[half][:, 0:512].rearrange("p (k t) -> p k t", k=4)
                dst = xnT[:, 4 * half:4 * half + 4, tok0:tok0 + 128]
                if half == 0:
                    fw.op("act", lambda: nc.scalar.copy(out=dst, in_=src), [RpTh[half]], [xnres])
                else:
                    fw.op("dve", lambda: nc.vector.tensor_copy(out=dst, in_=src), [RpTh[half]], [xnres])

        with contextlib.ExitStack() as ph:
            Wab = sbuf(ph, "Wab", [128, 8, 3200], BF16)
            stg = sbuf(ph, "p1stg", [128, 3072], F32)
            for kc in range(8):
                pieces = []
                g = gains[:, 0, kc:kc + 1]

                def pc(sc, dc, w):
                    pieces.append((sc, Wab[:, kc, dc:dc + w], w, R["Wab"]))
                pc(0, 0, 512)
                pc(512, 512, 64); pc(512, 576, 64); pc(576, 640, 64); pc(576, 704, 64)
                for i3 in range(3):
                    pc(768 + 768 * i3, 768 + 512 * i3, 512)
                pc(640, 2304, 128)
                for i3 in range(3):
                    pc(768 + 768 * i3 + 512, 2432 + 256 * i3, 256)
                load_cast(stg, R["p1stg"], w_in_ab[kc * 128:(kc + 1) * 128, :], 3072, pieces, gain_ap=g)
            xg = [sbuf(ph, "p1xg%d" % i, [128, 4, D], F32) for i in range(2)]
            junk = sbuf(ph, "p1junk", [128, D], BF16)
            xs = [sbuf(ph, "p1xs%d" % i, [128, D], BF16) for i in range(12)]
            ssq = [sbuf(ph, "p1ss%d" % i, [128, 4], F32) for i in range(3)]
            xnT = [sbuf(ph, "p1xnT%d" % i, [128, 8, 512], BF16) for i in range(2)]
            qst = [sbuf(ph, "p1qst%d" % i, [128, 3, 512], BF16) for i in range(3)]
            vst = [sbuf(ph, "p1vst%d" % i, [128, 4, 896], BF16) for i in range(2)]
            K_CHUNKS = [4, 5, 8, 9, 12, 13, 16, 17]
            pad_groups = {0, 1, 10, 11, 28, 29}
            cnt1 = {"nb": 0, "nq": 0}
            NG1 = NT0 // 512

            def p1_s1(gi):
                sq = ssq[gi % 3]
                sqr = R["p1ss%d" % (gi % 3)]
                xgt = xg[gi % 2]
                xr = R["p1xg%d" % (gi % 2)]
                fw.dma("sp", xgt[:, :, :], xin[gi * 512:(gi + 1) * 512, :].rearrange("(t p) f -> p t f", p=128), writes=[xr])
                for t in range(4):
                    sumsq(xgt[:, t, :], xr, junk[:], R["p1junk"], sq[:, t:t + 1], sqr)
                    rstd_from_ss(sq[:, t:t + 1], sqr, sq[:, t:t + 1], sqr, D)
                    xi = (gi % 3) * 4 + t
                    scale_only(xgt[:, t, :], xr, sq[:, t:t + 1], sqr, xs[xi], R["p1xs%d" % xi])

            def p1_s2(gi):
                for t in range(4):
                    xi = (gi % 3) * 4 + t
                    transpose_only(xs[xi], R["p1xs%d" % xi], xnT[gi % 2], R["p1xnT%d" % (gi % 2)], t * 128)

            def p1_s2_tile(gi, t):
                xi = (gi % 3) * 4 + t
                transpose_only(xs[xi], R["p1xs%d" % xi], xnT[gi % 2], R["p1xnT%d" % (gi % 2)], t * 128)

            def p1_s3(gi, nxt=None):
                xn = xnT[gi % 2]
                xnr = R["p1xnT%d" % (gi % 2)]
                chunks = K_CHUNKS if gi in pad_groups else list(range(18))
                step = 4 if len(chunks) == 18 else 2
                runs = []
                for c in chunks:
                    if runs and runs[-1][-1] == c - 1 and len(runs[-1]) < 3:
                        runs[-1].append(c)
                    else:
                        runs.append([c])
                ci = 0
                for run in runs:
                    qs = qst[cnt1["nq"] % 3]
                    qsr = R["p1qst%d" % (cnt1["nq"] % 3)]
                    cnt1["nq"] += 1
                    for k_, c in enumerate(run):
                        if nxt is not None and ci % step == 1 and ci // step < 4:
                            p1_s2_tile(nxt, ci // step)
                        ci += 1
                        bk = cnt1["nb"] % 4
                        cnt1["nb"] += 1
                        for kc in range(8):
                            mm(banks[bk][:, :], Wab[:, kc, c * 128:(c + 1) * 128], xn[:, kc, :], kc == 0, kc == 7,
                               [R["Wab"], xnr], [BK[bk]])
                        copy_rr(qs[:, k_, :], banks[bk][:, :], [BK[bk]], [qsr])
                    fw.dma("pool", QK0s[run[0]:run[0] + len(run), :, gi * 512:(gi + 1) * 512].rearrange("c p t -> p c t"),
                           qs[:, 0:len(run), :], reads=[qsr], writes=[R["QK0s"]])
                vs = vst[gi % 2]
                vsr = R["p1vst%d" % (gi % 2)]
                for t in range(4):
                    for (c0, w) in ((0, 512), (512, 384)):
                        bk = 4 + (cnt1["nb"] % 2)
                        cnt1["nb"] += 1
                        for kc in range(8):
                            mm(banks[bk][:, 0:w], xn[:, kc, t * 128:(t + 1) * 128], Wab[:, kc, 2304 + c0:2304 + c0 + w],
                               kc == 0, kc == 7, [R["Wab"], xnr], [BK[bk]])
                        copy_rr(vs[:, t, c0:c0 + w], banks[bk][:, 0:w], [BK[bk]], [vsr])
                fw.dma("pool", V0s[gi * 512:(gi + 1) * 512, :].rearrange("(t p) f -> p t f", p=128), vs[:, :, :], reads=[vsr],
                       writes=[R["V0s"]])

            p1_s1(0)
            p1_s1(1)
            p1_s2(0)
            for gi in range(NG1):
                if gi + 2 < NG1:
                    p1_s1(gi + 2)
                p1_s3(gi, nxt=(gi + 1 if gi + 1 < NG1 else None))
        fw.barrier()

        class Pipe:
            def __init__(self):
                self.steps = []
                self.deferred = []
                self.tick = 0

            def defer(self, delay, fn):
                self.deferred.append([self.tick + delay, fn])

            def run_deferred(self, flush=False):
                keep = []
                for it in self.deferred:
                    if flush or it[0] <= self.tick:
                        it[1]()
                    else:
                        keep.append(it)
                self.deferred = keep

            def add(self, qk_fn, pv_fn):
                self.steps.append((qk_fn, pv_fn))

            def run(self, look=2):
                n = len(self.steps)
                for k in range(n + look):
                    if k < n:
                        self.steps[k][0]()
                    if k >= look:
                        self.steps[k - look][1]()
                    self.tick += 1
                    self.run_deferred()
                self.steps = []

            def flush(self):
                while self.deferred:
                    self.tick += 1
                    self.run_deferred()

        with contextlib.ExitStack() as ph:
            biasA = sbuf(ph, "biasA_sb", [128, 8 * 384], BF16)
            biasB = sbuf(ph, "biasB_sb", [128, 12 * 512], BF16)
            Kbuf = [sbuf(ph, "Kbuf%d" % i, [128, 2, 6144], BF16) for i in range(2)]
            stgf = Kbuf[0][:, :, :].rearrange("p a b -> p (a b)").bitcast(F32)
            fw.dma("sp", stgf[:, 0:8 * 384], biasA_d, writes=[R["p2stgf"]])
            fw.op("dve", lambda: nc.vector.tensor_copy(out=biasA[:], in_=stgf[:, 0:8 * 384]), [R["p2stgf"]], [R["biasA"]])
            for hb in range(2):
                fw.dma("sp", stgf[:, 0:3072], biasB_d[:, hb * 3072:(hb + 1) * 3072], writes=[R["p2stgf"]])
                fw.op("dve", lambda: nc.vector.tensor_copy(out=biasB[:, hb * 3072:(hb + 1) * 3072], in_=stgf[:, 0:3072]), [R["p2stgf"]], [R["biasB"]])
            es = sbuf(ph, "es", [128, 8], F32)
            fw.dma("sp", es[:], sink_a.partition_broadcast(128), writes=[R["es"]])
            fw.op("act", lambda: nc.scalar.activation(out=es[:], in_=es[:], func=AF.Exp), [R["es"]], [R["es"]])

            Qbuf = [sbuf(ph, "Qbuf%d" % i, [128, 4096], BF16) for i in range(3)]
            for i in range(2):
                fw.op("pool", lambda: nc.gpsimd.memset(Kbuf[i][64:128, 0, :], 0.0), [R["biasB"], R["biasA"]], [R["Kbuf%d" % i], R["p2stgf"]])
                fw.op("pool", lambda: nc.gpsimd.memset(Kbuf[i][0:64, 1, :], 0.0), [R["biasB"], R["biasA"]], [R["Kbuf%d" % i], R["p2stgf"]])
            Vbuf = [sbuf(ph, "Vbuf%d" % i, [128, 2, 48, 128], BF16) for i in range(2)]
            acc = [sbuf(ph, "acc%d" % i, [128, 4096], F32) for i in range(2)]
            PT = sbuf(ph, "PT", [128, 8, 512], BF16)
            ost = [sbuf(ph, "ost%d" % i, [128, 512], BF16) for i in range(4)]
            Dt = [sbuf(ph, "Dt%d" % i, [128, 512], F32) for i in range(3)]
            RDs = [sbuf(ph, "RDs%d" % i, [128, 512], F32) for i in range(3)]
            Ot = [sbuf(ph, "Ot%d" % i, [128, 512], F32) for i in range(3)]
            for i in range(2):
                for par in range(2):
                    fw.op("pool", lambda: nc.gpsimd.memset(Vbuf[i][:, par, :, (1 - par) * 64:(1 - par) * 64 + 64], 1.0),
                          [], [R["Vbuf%d_%d" % (i, par)]])
            pipe = Pipe()
            ctr = {"st": 0, "pt": 0, "ob": 0, "fin": 0, "ost": 0, "k": 0, "q": 0, "v": 0}

            def finalize(src_ap, src_res, width, par, es_ap, dst_dram, release_first=True):
                oh, dh = (0, 64) if par == 0 else (64, 0)
                f = ctr["fin"] % 3
                ctr["fin"] += 1
                dt, rd, ot = Dt[f], RDs[f], Ot[f]
                dtr, rdr, otr = R["Dt%d" % f], R["RDs%d" % f], R["Ot%d" % f]
                if es_ap is not None:
                    fw.op("act", lambda: nc.scalar.activation(out=dt[dh:dh + 64, 0:width], in_=src_ap[dh:dh + 64, 0:width], func=AF.Ln,
                                                               bias=es_ap[dh:dh + 64, :], scale=1.0),
                          [src_res, R["es"]], [dtr])
                else:
                    fw.op("act", lambda: nc.scalar.activation(out=dt[dh:dh + 64, 0:width], in_=src_ap[dh:dh + 64, 0:width], func=AF.Ln),
                          [src_res], [dtr])
                fw.op("dve", lambda: nc.vector.tensor_copy(out=ot[oh:oh + 64, 0:width], in_=src_ap[oh:oh + 64, 0:width]),
                      [src_res], [otr])
                fw.op("act", lambda: nc.scalar.activation(out=dt[dh:dh + 64, 0:width], in_=dt[dh:dh + 64, 0:width], func=AF.Exp, scale=-1.0),
                      [dtr], [dtr])

                def stage2():
                    fw.op("pe", lambda: nc.tensor.matmul(banks[6][oh:oh + 64, 0:width], lhsT=identF[dh:dh + 64, dh:dh + 64],
                                                          rhs=dt[dh:dh + 64, 0:width], start=True, stop=True),
                          [dtr, R["identF"]], [BK[6]])
                    fw.op("act", lambda: nc.scalar.copy(out=rd[oh:oh + 64, 0:width], in_=banks[6][oh:oh + 64, 0:width]),
                          [BK[6]], [rdr])
                    o = ctr["ost"] % 4
                    ctr["ost"] += 1
                    fw.op("dve", lambda: nc.vector.tensor_tensor(out=ost[o][oh:oh + 64, 0:width], in0=ot[oh:oh + 64, 0:width],
                                                                  in1=rd[oh:oh + 64, 0:width], op=ALU.mult),
                          [otr, rdr], [R["ost%d" % o]])
                    fw.dma("pool", dst_dram, ost[o][oh:oh + 64, 0:width], reads=[R["ost%d" % o]], writes=[R["Odram"]])
                pipe.defer(10, stage2)

            def make_item(Kb, Kres, Qb, Qres, half, chunks, ob, d, bias_list, Vt, Vres, on_done):
                nchunks = len(chunks)
                pts = []
                for ci, ch in enumerate(chunks):
                    pts.append(ctr["pt"] % 8)
                    ctr["pt"] += 1
                width = max(ch["qoff"] + ch["n"] for ch in chunks)
                blocks = []
                for off in range(0, width, 128):
                    cov = [(ci, off - ch["qoff"]) for ci, ch in enumerate(chunks) if ch["qoff"] <= off < ch["qoff"] + ch["n"]]
                    blocks.append((off, cov))
                ready = {}
                for bi_, (off, cov) in enumerate(blocks):
                    ready.setdefault(max(ci for ci, _ in cov), []).append(bi_)
                for ci, ch in enumerate(chunks):
                    st_i = ctr["st"] % 3
                    ctr["st"] += 1
                    pt_i = pts[ci]
                    n = ch["n"]

                    def qk(ch=ch, st_i=st_i, pt_i=pt_i, n=n):
                        k0 = ch["kcol0"]
                        q0 = ch["qcol0"]
                        lhsT = Kb[:, half, k0:k0 + 127 * d + 1:d]
                        rhs = Qb[:, q0:q0 + (n - 1) * d + 1:d]
                        nb_ = len(bias_list)
                        mm(banks[st_i][:, 0:n], lhsT, rhs, True, nb_ == 0, [Kres, Qres], [BK[st_i]])
                        for bi, (btile, bres, boff) in enumerate(bias_list):
                            mm(banks[st_i][:, 0:n], identB[:], btile[:, boff + ch["bias_c0"]:boff + ch["bias_c0"] + n],
                               False, bi == nb_ - 1, [R["identB"], bres], [BK[st_i]])
                        mk = ch["mask"]
                        if mk is None:
                            fw.op("act", lambda: nc.scalar.activation(out=PT[:, pt_i, 0:n], in_=banks[st_i][:, 0:n], func=AF.Exp,
                                                                       scale=0.125),
                                  [BK[st_i]], [R["PT%d" % pt_i]])
                        else:
                            fw.op("act", lambda: nc.scalar.activation(out=PT[:, pt_i, 0:n], in_=banks[st_i][:, 0:n], func=AF.Exp,
                                                                       bias=maskt[:, mk:mk + 1], scale=0.125),
                                  [BK[st_i], R["maskt"]], [R["PT%d" % pt_i]])

                    def pv(ci=ci):
                        for bi_ in ready.get(ci, []):
                            off, cov = blocks[bi_]
                            for k_, (cj, pc) in enumerate(cov):
                                mm(banks[3 + ob][:, off:off + 128], Vt[:, chunks[cj]["vtile"], :], PT[:, pts[cj], pc:pc + 128],
                                   k_ == 0, k_ == len(cov) - 1, [Vres, R["PT%d" % pts[cj]]], [BK[3 + ob]])
                        if ci == nchunks - 1:
                            on_done()
                    pipe.add(qk, pv)

            for si in range(3):
                s0 = SEG_START[si]
                c0 = si * SEGL
                mcol = si * 4
                for g in range(2):
                    kb_i = ctr["k"] % 2
                    ctr["k"] += 1
                    Kb, Kres = Kbuf[kb_i], R["Kbuf%d" % kb_i]
                    for e_ in range(2):
                        fw.dma("sp", Kb[e_ * 64:e_ * 64 + 64, e_, 0:4352], QK0s[4 + g, e_ * 64:e_ * 64 + 64, s0 - 128:s0 + 4096 + 128],
                               writes=[Kres])
                    vb_i = ctr["v"] % 2
                    ctr["v"] += 1
                    Vb = Vbuf[vb_i]
                    for par in range(2):
                        for t0 in range(0, 34, 17):
                            src = V0s[s0 - 128 + t0 * 128:s0 - 128 + (t0 + 17) * 128, g * 64:g * 64 + 64]
                            fw.dma("sp", Vb[:, par, t0:t0 + 17, par * 64:par * 64 + 64],
                                   src.rearrange("(c k) f -> k c f", k=128), writes=[R["Vbuf%d_%d" % (vb_i, par)]])
                    for qc in (2 * g, 2 * g + 1):
                        qb_i = ctr["q"] % 3
                        ctr["q"] += 1
                        Qb, Qres = Qbuf[qb_i], R["Qbuf%d" % qb_i]
                        fw.dma("sp", Qb[:], QK0s[qc, :, s0:s0 + 4096], writes=[Qres])
                        for e in range(2):
                            h = 2 * qc + e
                            for m in range(8):
                                ob = ctr["ob"] % 3
                                ctr["ob"] += 1
                                chunks = []
                                for c in range(4 * m - 1, 4 * m + 5):
                                    blocks = [b for b in (c - 1, c, c + 1) if 4 * m <= b < 4 * m + 4]
                                    if not blocks:
                                        continue
                                    b0 = blocks[0]
                                    mk = None
                                    if c == -1:
                                        mk = mcol + 2
                                    elif c == 32:
                                        mk = mcol + 3
                                    chunks.append(dict(kcol0=128 + 128 * c, qcol0=128 * b0, n=128 * len(blocks),
                                                       qoff=128 * (b0 - 4 * m), bias_c0=128 * (b0 - (c - 1)), mask=mk,
                                                       vtile=c + 1))
                                dst = O0s[qc, e * 64:e * 64 + 64, c0 + m * 512:c0 + (m + 1) * 512]

                                def done(ob=ob, e=e, h=h, dst=dst):
                                    finalize(banks[3 + ob], BK[3 + ob], 512, e, es[:, h:h + 1], dst)
                                Vt = Vb[:, e, :, :]
                                make_item(Kb, Kres, Qb, Qres, e, chunks, ob, 1, [(biasA, R["biasA"], h * 384)], Vt,
                                          R["Vbuf%d_%d" % (vb_i, e)], done)
                        pipe.run()
                for hp in range(2):
                    for i3, d in enumerate((1, 4, 16)):
                        Ls = SEGL // d
                        Hd = 64 * d
                        nchk = Ls // 128 + 1
                        kb_i = ctr["k"] % 2
                        ctr["k"] += 1
                        Kb, Kres = Kbuf[kb_i], R["Kbuf%d" % kb_i]
                        for e_ in range(2):
                            fw.dma("sp", Kb[e_ * 64:e_ * 64 + 64, e_, 0:4096 + 2 * Hd],
                                   QK0s[6 + 4 * i3 + 2 + hp, e_ * 64:e_ * 64 + 64, s0 - Hd:s0 + 4096 + Hd], writes=[Kres])
                        qb_i = ctr["q"] % 3
                        ctr["q"] += 1
                        Qb, Qres = Qbuf[qb_i], R["Qbuf%d" % qb_i]
                        fw.dma("sp", Qb[:], QK0s[6 + 4 * i3 + hp, :, s0:s0 + 4096], writes=[Qres])
                        vb_i = ctr["v"] % 2
                        ctr["v"] += 1
                        Vb = Vbuf[vb_i]
                        for e in range(2):
                            hB = 2 * hp + e
                            vcol = 128 + 256 * i3 + 64 * hB
                            a00 = s0 - Hd
                            if d == 16:
                                for c in range(nchk):
                                    src = V0s[a00 + d * 128 * c:a00 + d * 128 * (c + 1), vcol:vcol + 64].rearrange("(k r) f -> k r f", r=d)
                                    dst = Vb[:, e, c:c + (d - 1) * nchk + 1:nchk, e * 64:e * 64 + 64]
                                    fw.dma("sp", dst, src, writes=[R["Vbuf%d_%d" % (vb_i, e)]])
                            else:
                                for r in range(d):
                                    a0 = a00 + r
                                    tstep = 11 if nchk > 11 else nchk
                                    for t0 in range(0, nchk, tstep):
                                        tn = min(tstep, nchk - t0)
                                        src = V0s[a0 + d * 128 * t0:a0 + d * (128 * (t0 + tn) - 1) + 1:d, vcol:vcol + 64]
                                        fw.dma("sp", Vb[:, e, r * nchk + t0:r * nchk + t0 + tn, e * 64:e * 64 + 64],
                                               src.rearrange("(c k) f -> k c f", k=128),
                                               writes=[R["Vbuf%d_%d" % (vb_i, e)]])
                        for e in range(2):
                            hB = 2 * hp + e
                            hidx = 4 * i3 + hB
                            BW = min(512, Ls)
                            nbk = Ls // BW
                            bpb = BW // 128
                            for r in range(d):
                                for m in range(nbk):
                                    ob = ctr["ob"] % 3
                                    ctr["ob"] += 1
                                    chunks = []
                                    for c in range(bpb * m, bpb * m + bpb + 1):
                                        blocks = [b for b in (c - 1, c) if bpb * m <= b < bpb * m + bpb]
                                        if not blocks:
                                            continue
                                        b0 = blocks[0]
                                        mk = None
                                        if c == 0:
                                            mk = mcol + 0
                                        elif c == nchk - 1:
                                            mk = mcol + 1
                                        chunks.append(dict(kcol0=r + d * (128 * c), qcol0=r + d * 128 * b0, n=128 * len(blocks),
                                                           qoff=128 * (b0 - bpb * m), bias_c0=128 * (b0 - (c - 1)), mask=mk,
                                                           vtile=r * nchk + c))

                                    def done(ob=ob, e=e, r=r, m=m, d=d, BW=BW, i3=i3):
                                        a = acc[e]
                                        ar = R["acc%d" % e]
                                        dstv = a[:, r + d * m * BW:r + d * (m * BW + BW - 1) + 1:d]
                                        if i3 == 0:
                                            fw.op("dve", lambda: nc.vector.tensor_copy(out=dstv, in_=banks[3 + ob][:, 0:BW]),
                                                  [BK[3 + ob]], [ar])
                                        else:
                                            fw.op("dve", lambda: nc.vector.tensor_tensor(out=dstv, in0=dstv, in1=banks[3 + ob][:, 0:BW],
                                                                                          op=ALU.add),
                                                  [BK[3 + ob], ar], [ar])
                                    Vt = Vb[:, e, :, :]
                                    make_item(Kb, Kres, Qb, Qres, e, chunks, ob, d,
                                              [(biasB, R["biasB"], hidx * 512), (biasB, R["biasB"], hidx * 512 + 256)], Vt,
                                              R["Vbuf%d_%d" % (vb_i, e)], done)
                        pipe.run()
                    pipe.flush()
                    nfin = 0
                    for e in range(2):
                        for m in range(8):
                            dst = O0s[4 + hp, e * 64:e * 64 + 64, c0 + m * 512:c0 + (m + 1) * 512]
                            finalize(acc[e][:, m * 512:(m + 1) * 512], R["acc%d" % e], 512, e, None, dst)
                            nfin += 1
                            if nfin % 3 == 0:
                                pipe.flush()
                    pipe.flush()
            pipe.flush()
        fw.barrier()

        def w13_load(l, j, w13, pfx):
            fw.dma("sp", w13[j % 3][:], W13s[l, j], writes=[R[pfx + "w13_%d" % (j % 3)]])

        def ffn_block(ph, l, xt4, xres4, xnT, xnres, W2r, gT, w13, pfx, hook=None, down_banks=3, after_up=None):
            for j in range(NJ):
                wi = j % 3
                wres = R[pfx + "w13_%d" % wi]
                b1, b3 = (0, 1) if j % 2 == 0 else (2, 3)
                for kc in range(8):
                    mm(banks[b1][:, :], w13[wi][:, kc, 0:128], xnT[:, kc, :], kc == 0, kc == 7, [wres, xnres], [BK[b1]])
                for kc in range(8):
                    mm(banks[b3][:, :], w13[wi][:, kc, 128:256], xnT[:, kc, :], kc == 0, kc == 7, [wres, xnres], [BK[b3]])
                if j + 3 < NJ:
                    w13_load(l, j + 3, w13, pfx)
                sl = silu_t[j % 2]
                slr = R[pfx + "silu%d" % (j % 2)]
                fw.op("act", lambda: nc.scalar.activation(out=sl[:], in_=banks[b1][:, :], func=AF.Silu), [BK[b1]], [slr])
                fw.op("dve", lambda: nc.vector.tensor_tensor(out=gT[:, j, :], in0=sl[:], in1=banks[b3][:, :], op=ALU.mult),
                      [slr, BK[b3]], [R[pfx + "gT"]])
                if hook is not None:
                    hook(j)
            if after_up is not None:
                after_up()
            nb = 0
            for t in range(4):
                for hf in range(2):
                    bk = 4 + (nb % down_banks)
                    for j in range(NJ):
                        mm(banks[bk][:, :], gT[:, j, t * 128:(t + 1) * 128], W2r[:, j, hf * 512:(hf + 1) * 512], j == 0, j == NJ - 1,
                           [R[pfx + "gT"], R[pfx + "W2r"]], [BK[bk]])
                    fw.op("dve", lambda: nc.vector.tensor_tensor(out=xt4[t][:, hf * 512:(hf + 1) * 512],
                                                                  in0=xt4[t][:, hf * 512:(hf + 1) * 512],
                                                                  in1=banks[bk][:, :], op=ALU.add),
                          [BK[bk], xres4[t]], [xres4[t]])
                    if hook is not None:
                        hook(NJ + nb)
                    nb += 1

        def load_W2r(ph, l, W2r, stg, stgres, pfx):
            for j in range(NJ):
                fw.dma("sp", stg[:, 0:D], ffn_w2[l, j * 128:(j + 1) * 128, :], writes=[stgres])
                e = "dve" if j % 2 == 0 else "pool"
                engh = nc.vector if j % 2 == 0 else nc.gpsimd
                fw.op(e, lambda: engh.tensor_copy(out=W2r[:, j, :], in_=stg[:, 0:D]), [stgres], [R[pfx + "W2r"]])

        def outproj_residual(oT, ores, nkc, Wo, wres, xt4, xres4, tiles=(0, 1, 2, 3)):
            nb = 0
            for t in tiles:
                for hf in range(2):
                    bk = nb % 2
                    nb += 1
                    for kc in range(nkc):
                        mm(banks[bk][:, :], oT[:, kc, t * 128:(t + 1) * 128], Wo[:, kc, hf * 512:(hf + 1) * 512], kc == 0,
                           kc == nkc - 1, [ores, wres], [BK[bk]])
                    fw.op("dve", lambda: nc.vector.tensor_tensor(out=xt4[t][:, hf * 512:(hf + 1) * 512],
                                                                  in0=xt4[t][:, hf * 512:(hf + 1) * 512],
                                                                  in1=banks[bk][:, :], op=ALU.add),
                          [BK[bk], xres4[t]], [xres4[t]])

        with contextlib.ExitStack() as ph:
            pfx = "p3"
            stg = sbuf(ph, "p3stg", [128, 1536], F32)
            stgres = R["p3stg"]
            W2r = sbuf(ph, "p3W2r", [128, NJ, D], BF16)
            load_W2r(ph, 0, W2r, stg, stgres, pfx)
            Wo = sbuf(ph, "p3Wo", [128, 6, D], BF16)
            for kc in range(6):
                load_cast(stg, stgres, w_out_ab[kc * 128:(kc + 1) * 128, :], D, [(0, Wo[:, kc, :], D, R["p3Wo"])])
            Wc = sbuf(ph, "p3Wc", [128, 8, 1536], BF16)
            for kc in range(8):
                load_cast(stg, stgres, w_in_c[kc * 128:(kc + 1) * 128, :], 1536, [(0, Wc[:, kc, :], 1536, R["p3Wc"])],
                          gain_ap=gains[:, 1, kc:kc + 1])
            gainqk = sbuf(ph, "gainqk", [128, 128], F32)
            fw.dma("sp", gainqk[:, 0:64], q_gain.partition_broadcast(128), writes=[R["gainqk"]])
            fw.dma("sp", gainqk[:, 64:128], k_gain.partition_broadcast(128), writes=[R["gainqk"]])
            w13 = [sbuf(ph, "p3w13_%d" % i, [128, 8, 256], BF16) for i in range(3)]
            silu_t = [sbuf(ph, "p3silu%d" % i, [128, 512], F32) for i in range(2)]
            gT = sbuf(ph, "p3gT", [128, NJ, 512], BF16)
            oT = [sbuf(ph, "p3oT%d" % i, [128, 6, 512], BF16) for i in range(1)]
            xtg3 = sbuf(ph, "p3xtg", [128, 4, D], F32)
            junk = sbuf(ph, "p3junk", [128, D], BF16)
            xs4 = [sbuf(ph, "p3xs4_%d" % i, [128, D], BF16) for i in range(4)]
            ssq = [sbuf(ph, "p3ss%d" % i, [128, 8], F32) for i in range(2)]
            xnT = [sbuf(ph, "p3xnT%d" % i, [128, 8, 512], BF16) for i in range(1)]
            qn = [sbuf(ph, "p3qn%d" % i, [128, 1280], F32) for i in range(2)]
            ssh = [sbuf(ph, "p3ssh%d" % i, [128, 20], F32) for i in range(2)]
            cs = [sbuf(ph, "p3cs%d" % i, [128, 64], F32) for i in range(2)]
            rt = [sbuf(ph, "p3rt%d" % i, [128, 640], F32) for i in range(4)]
            qr = [sbuf(ph, "p3qr%d" % i, [128, 1280], BF16) for i in range(2)]
            qTs = [sbuf(ph, "p3qTs%d" % i, [128, 10, 128], BF16) for i in range(2)]
            vs1 = [sbuf(ph, "p3vs%d" % i, [128, 256], BF16) for i in range(2)]
            xn1T = sbuf(ph, "p3xn1T", [128, 8, 512], BF16)
            xn1r = R["p3xn1T"]
            BQ = (4, 5, 6)
            tcnt = {"n": 0}

            def make_B(gi):
                si = gi // 8
                cbase = gi * 512
                own = si in OWN
                obase = OWN[si] + (gi % 8) * 512 if own else None
                tiles = []
                for t in range(4):
                    tc = tcnt["n"]
                    tcnt["n"] += 1
                    p2 = tc % 2
                    ctok = cbase + t * 128
                    blist = [0, 1, 2] if own else [2]
                    h0 = 0 if own else 16
                    nh = 20 - h0
                    qn_t, qn_r = qn[p2], R["p3qn%d" % p2]
                    ssh_t, ssh_r = ssh[p2], R["p3ssh%d" % p2]
                    v_t, v_r = vs1[p2], R["p3vs%d" % p2]
                    cs_t, cs_r = cs[p2], R["p3cs%d" % p2]
                    qr_t, qr_r = qr[p2], R["p3qr%d" % p2]
                    qT_t, qT_r = qTs[p2], R["p3qTs%d" % p2]

                    def st0(t=t, blist=blist, qn_t=qn_t, qn_r=qn_r, cs_t=cs_t, cs_r=cs_r, ctok=ctok):
                        for b in blist:
                            for kc in range(8):
                                mm(banks[BQ[b]][:, :], xn1T[:, kc, t * 128:(t + 1) * 128], Wc[:, kc, b * 512:(b + 1) * 512], kc == 0, kc == 7,
                                   [xn1r, R["p3Wc"]], [BK[BQ[b]]])
                        for b in blist:
                            w = 512 if b < 2 else 256
                            fw.op("act", lambda: nc.scalar.activation(out=qn_t[:, b * 512:b * 512 + w], in_=banks[BQ[b]][:, 0:w],
                                                                       func=AF.Square),
                                  [BK[BQ[b]]], [qn_r])
                        fw.dma("sp", cs_t[:, 0:32], cos_d[ctok:ctok + 128, :], writes=[cs_r])
                        fw.dma("sp", cs_t[:, 32:64], sin_d[ctok:ctok + 128, :], writes=[cs_r])

                    def st1(h0=h0, qn_t=qn_t, qn_r=qn_r, ssh_t=ssh_t, ssh_r=ssh_r, v_t=v_t, v_r=v_r, ctok=ctok):
                        fw.op("dve", lambda: nc.vector.tensor_reduce(out=ssh_t[:, h0:20],
                                                                     in_=qn_t[:, h0 * 64:1280].rearrange("p (h d) -> p h d", d=64),
                                                                     op=ALU.add, axis=AX.X),
                              [qn_r], [ssh_r])
                        fw.op("act", lambda: nc.scalar.activation(out=ssh_t[:, h0:20], in_=ssh_t[:, h0:20], func=AF.Sqrt,
                                                                   bias=epst[:, 0:1], scale=1.0 / 64),
                              [ssh_r, R["epst"]], [ssh_r])
                        fw.op("act", lambda: nc.scalar.copy(out=v_t[:], in_=banks[BQ[2]][:, 256:512]), [BK[BQ[2]]], [v_r])
                        if ctok < 4096:
                            for g_ in range(4):
                                fw.dma("pool", V1samp[g_][ctok:ctok + 128, :], v_t[:, g_ * 64:(g_ + 1) * 64], reads=[v_r], writes=[R["V1s"]])
                        else:
                            fw.dma("pool", V1p[ctok - 4096:ctok - 4096 + 128, :], v_t[:], reads=[v_r], writes=[R["V1s"]])

                    def st2(h0=h0, blist=blist, qn_t=qn_t, qn_r=qn_r, ssh_t=ssh_t, ssh_r=ssh_r):
                        fw.op("dve", lambda: nc.vector.reciprocal(out=ssh_t[:, h0:20], in_=ssh_t[:, h0:20]), [ssh_r], [ssh_r])
                        for b in blist:
                            w = 512 if b < 2 else 256
                            nhb = w // 64
                            fw.op("dve", lambda: nc.vector.tensor_tensor(
                                out=qn_t[:, b * 512:b * 512 + w].rearrange("p (h d) -> p h d", d=64),
                                in0=banks[BQ[b]][:, 0:w].rearrange("p (h d) -> p h d", d=64),
                                in1=ssh_t[:, b * 8:b * 8 + nhb].unsqueeze(2).to_broadcast([128, nhb, 64]), op=ALU.mult),
                                [BK[BQ[b]], ssh_r, qn_r], [qn_r])
                        if h0 == 0:
                            fw.op("pool", lambda: nc.gpsimd.tensor_tensor(
                                out=qn_t[:, 0:1024].rearrange("p (h d) -> p h d", d=64),
                                in0=qn_t[:, 0:1024].rearrange("p (h d) -> p h d", d=64),
                                in1=gainqk[:, 0:64].unsqueeze(1).to_broadcast([128, 16, 64]), op=ALU.mult),
                                [qn_r, R["gainqk"]], [qn_r])
                        fw.op("pool", lambda: nc.gpsimd.tensor_tensor(
                            out=qn_t[:, 1024:1280].rearrange("p (h d) -> p h d", d=64),
                            in0=qn_t[:, 1024:1280].rearrange("p (h d) -> p h d", d=64),
                            in1=gainqk[:, 64:128].unsqueeze(1).to_broadcast([128, 4, 64]), op=ALU.mult),
                            [qn_r, R["gainqk"]], [qn_r])

                    def st3(h0=h0, nh=nh, qn_t=qn_t, qn_r=qn_r, cs_t=cs_t, cs_r=cs_r):
                        qv = qn_t[:, h0 * 64:1280].rearrange("p (h i two) -> p h i two", i=32, two=2)
                        x0, x1 = qv[:, :, :, 0], qv[:, :, :, 1]
                        cb = cs_t[:, 0:32].unsqueeze(1).to_broadcast([128, nh, 32])
                        sb_ = cs_t[:, 32:64].unsqueeze(1).to_broadcast([128, nh, 32])
                        r4 = [rt[i][:, 0:nh * 32].rearrange("p (h i) -> p h i", i=32) for i in range(4)]
                        rr4 = [R["p3rt%d" % i] for i in range(4)]
                        fw.op("pool", lambda: nc.gpsimd.tensor_tensor(out=r4[1], in0=x1, in1=sb_, op=ALU.mult), [qn_r, cs_r], [rr4[1]])
                        fw.op("pool", lambda: nc.gpsimd.tensor_tensor(out=r4[3], in0=x1, in1=cb, op=ALU.mult), [qn_r, cs_r], [rr4[3]])
                        fw.op("dve", lambda: nc.vector.tensor_tensor(out=r4[0], in0=x0, in1=cb, op=ALU.mult), [qn_r, cs_r], [rr4[0]])
                        fw.op("dve", lambda: nc.vector.tensor_tensor(out=r4[2], in0=x0, in1=sb_, op=ALU.mult), [qn_r, cs_r], [rr4[2]])

                    def st3b(h0=h0, nh=nh, qr_t=qr_t, qr_r=qr_r):
                        r4 = [rt[i][:, 0:nh * 32].rearrange("p (h i) -> p h i", i=32) for i in range(4)]
                        rr4 = [R["p3rt%d" % i] for i in range(4)]
                        qrv = qr_t[:, h0 * 64:1280].rearrange("p (h i two) -> p h i two", i=32, two=2)
                        fw.op("dve", lambda: nc.vector.tensor_tensor(out=qrv[:, :, :, 0], in0=r4[0], in1=r4[1], op=ALU.subtract),
                              [rr4[0], rr4[1]], [qr_r])
                        fw.op("dve", lambda: nc.vector.tensor_tensor(out=qrv[:, :, :, 1], in0=r4[2], in1=r4[3], op=ALU.add),
                              [rr4[2], rr4[3]], [qr_r])

                    def st4(t=t, own=own, obase=obase, ctok=ctok, qr_t=qr_t, qr_r=qr_r, qT_t=qT_t, qT_r=qT_r):
                        if own:
                            for c in range(8):
                                fw.op("pe", lambda: nc.tensor.transpose(out=pT[:, c * 128:(c + 1) * 128], in_=qr_t[:, c * 128:(c + 1) * 128],
                                                                         identity=identB[:]),
                                      [qr_r, R["identB"]], [RpT], inc=(c == 7))
                            fw.op("act", lambda: nc.scalar.copy(out=qT_t[:, 0:8, :], in_=pT[:].rearrange("p (k t) -> p k t", k=8)),
                                  [RpT], [qT_r])
                            otok = obase + t * 128
                            fw.dma("pool", Q1s[:, :, otok:otok + 128].rearrange("c p t -> p c t"), qT_t[:, 0:8, :], reads=[qT_r],
                                   writes=[R["Q1s"]])
                        for c in range(8, 10):
                            fw.op("pe", lambda: nc.tensor.transpose(out=pT[:, (c - 8) * 128:(c - 7) * 128], in_=qr_t[:, c * 128:(c + 1) * 128],
                                                                     identity=identB[:]),
                                  [qr_r, R["identB"]], [RpT], inc=(c == 9))
                        fw.op("act", lambda: nc.scalar.copy(out=qT_t[:, 8:10, :], in_=pT[:, 0:256].rearrange("p (k t) -> p k t", k=2)),
                              [RpT], [qT_r])
                        for kv in range(4):
                            srcp = qT_t[(kv % 2) * 64:(kv % 2) * 64 + 64, 8 + kv // 2, :]
                            if ctok < 4096:
                                kdst = K1samp[kv][:, ctok:ctok + 128]
                            else:
                                kdst = K1p[kv, :, ctok - 4096:ctok - 4096 + 128]
                            fw.dma("pool", kdst, srcp, reads=[qT_r], writes=[R["K1s"]])
                    tiles.append([st0, st1, st2, st3, st3b, st4])
                return tiles

            def make_hook(Bt):
                sched = {}
                offs = (0, 1, 2, 3, 5, 9)
                for t in range(4):
                    for sidx in range(6):
                        sched.setdefault(5 * t + offs[sidx], []).append(Bt[t][sidx])

                def hook(j):
                    for fn in sched.get(j, []):
                        fn()
                return hook

            NG3 = NC_TOK // 512
            prevB = None
            for j_ in range(3):
                w13_load(0, j_, w13, pfx)
            for gi in range(NG3):
                si = gi // 8
                a0 = SEG_START[si] + (gi % 8) * 512
                cbase = gi * 512
                own = si in OWN
                obase = OWN[si] + (gi % 8) * 512 if own else None
                o_t = oT[0]
                o_r = R["p3oT0"]
                if gi == 0:
                    fw.dma("sp", o_t[:, :, :], O0s[:, :, cbase:cbase + 512].rearrange("c p t -> p c t"), writes=[o_r])
                xt4 = [xtg3[:, t, :] for t in range(4)]
                xres4 = [R["p3x%d" % t] for t in range(4)]
                fw.dma("sp", xtg3[:, :, :], xin[a0:a0 + 512, :].rearrange("(t p) f -> p t f", p=128), writes=xres4)
                xn = xnT[0]
                xnr = R["p3xnT0"]
                sq = ssq[gi % 2]
                sqr = R["p3ss%d" % (gi % 2)]
                for t in range(4):
                    outproj_residual(o_t, o_r, 6, Wo, R["p3Wo"], xt4, xres4, tiles=(t,))
                    sumsq(xt4[t][:], xres4[t], junk[:], R["p3junk"], sq[:, t:t + 1], sqr)
                    rstd_from_ss(sq[:, t:t + 1], sqr, sq[:, t:t + 1], sqr, D)
                    scale_only(xt4[t][:], xres4[t], sq[:, t:t + 1], sqr, xs4[t], R["p3xs4_%d" % t])
                for t in range(4):
                    transpose_only(xs4[t], R["p3xs4_%d" % t], xn, xnr, t * 128)
                post = {}

                def ap1(t, xt4=xt4, xres4=xres4, sq=sq, sqr=sqr, obase=obase):
                    fw.dma("pool", X1s[obase + t * 128:obase + (t + 1) * 128, :], xt4[t][:], reads=[xres4[t]], writes=[R["X1s"]])
                    sumsq(xt4[t][:], xres4[t], junk[:], R["p3junk"], sq[:, 4 + t:5 + t], sqr)
                    rstd_from_ss(sq[:, 4 + t:5 + t], sqr, sq[:, 4 + t:5 + t], sqr, D)
                    scale_only(xt4[t][:], xres4[t], sq[:, 4 + t:5 + t], sqr, xs4[t], R["p3xs4_%d" % t])

                def ap2(t):
                    transpose_only(xs4[t], R["p3xs4_%d" % t], xn1T, xn1r, t * 128)
                for t in range(4):
                    post.setdefault(NJ + 2 * t + 1, []).append(lambda t=t: ap1(t))
                    post.setdefault(NJ + 2 * t + 3, []).append(lambda t=t: ap2(t))
                bhook = make_hook(prevB) if prevB else None

                def hook(slot, bhook=bhook, post=post, gi=gi):
                    if slot == 8 and gi + 1 < NG3:
                        fw.dma("sp", oT[0][:, :, :], O0s[:, :, (gi + 1) * 512:(gi + 2) * 512].rearrange("c p t -> p c t"),
                               writes=[R["p3oT0"]])
                    if bhook is not None:
                        bhook(slot)
                    for fn in post.pop(slot, []):
                        fn()
                def after_up3(gi=gi):
                    if gi + 1 < NG3:
                        for j_ in range(3):
                            w13_load(0, j_, w13, pfx)
                ffn_block(ph, 0, xt4, xres4, xn, xnr, W2r, gT, w13, pfx, hook=hook, down_banks=2, after_up=after_up3)
                for slot in sorted(post):
                    for fn in post[slot]:
                        fn()
                prevB = make_B(gi)
            for t in range(4):
                for fn in prevB[t]:
                    fn()
        fw.barrier()
        CC_GROUPS = [[0, 1, 2, 3], [4, 5, 6, 7]]
        for g_ in range(4):
            fw.coll("AllGather", CC_GROUPS, K1samp[g_], K1g[g_], R["K1g%d" % g_])
            fw.coll("AllGather", CC_GROUPS, V1samp[g_], V1g[g_], R["V1g%d" % g_])

        with contextlib.ExitStack() as ph:
            K1b = sbuf(ph, "K1b", [128, 2, 16384], BF16)
            V1b = sbuf(ph, "V1b", [128, 2, 128, 128], BF16)
            Qb1 = [sbuf(ph, "Qb1_%d" % i, [128, 4096], BF16) for i in range(2)]
            PT = sbuf(ph, "PT4", [128, 6, 512], BF16)
            ost = [sbuf(ph, "ost4_%d" % i, [128, 512], BF16) for i in range(4)]
            Dt = [sbuf(ph, "Dt4_%d" % i, [128, 512], F32) for i in range(2)]
            RDs = [sbuf(ph, "RDs4_%d" % i, [128, 512], F32) for i in range(2)]
            Ot = [sbuf(ph, "Ot4_%d" % i, [128, 512], F32) for i in range(2)]
            for par in range(2):
                fw.op("pool", lambda: nc.gpsimd.memset(V1b[:, par, :, (1 - par) * 64:(1 - par) * 64 + 64], 1.0), [],
                      [R["V1b_%d" % par]])
            fw.op("pool", lambda: nc.gpsimd.memset(K1b[64:128, 0, :], 0.0), [], [R["K1b"]])
            fw.op("pool", lambda: nc.gpsimd.memset(K1b[0:64, 1, :], 0.0), [], [R["K1b"]])
            pipe = Pipe()
            ctr = {"st": 0, "ob": 0, "fin": 0, "ost": 0, "q": 0}
            STP = [PA, PB]
            STR = [R["PAres"], R["PBres"]]

            def finalize4(ob, par, dst_dram):
                oh, dh = (0, 64) if par == 0 else (64, 0)
                f = ctr["fin"] % 2
                ctr["fin"] += 1
                dt, rd, ot = Dt[f], RDs[f], Ot[f]
                dtr, rdr, otr = R["Dt4_%d" % f], R["RDs4_%d" % f], R["Ot4_%d" % f]
                src_ap, src_res = banks[4 + ob], BK[4 + ob]
                fw.op("dve", lambda: nc.vector.tensor_copy(out=dt[dh:dh + 64, :], in_=src_ap[dh:dh + 64, :]), [src_res], [dtr])
                fw.op("dve", lambda: nc.vector.tensor_copy(out=ot[oh:oh + 64, :], in_=src_ap[oh:oh + 64, :]), [src_res], [otr])
                fw.op("dve", lambda: nc.vector.reciprocal(out=dt[dh:dh + 64, :], in_=dt[dh:dh + 64, :]), [dtr], [dtr])

                def stage2():
                    fw.op("pe", lambda: nc.tensor.matmul(banks[6][oh:oh + 64, :], lhsT=identF[dh:dh + 64, dh:dh + 64],
                                                          rhs=dt[dh:dh + 64, :], start=True, stop=True),
                          [dtr, R["identF"]], [BK[6]])
                    fw.op("dve", lambda: nc.vector.tensor_copy(out=rd[oh:oh + 64, :], in_=banks[6][oh:oh + 64, :]), [BK[6]], [rdr])
                    o = ctr["ost"] % 4
                    ctr["ost"] += 1
                    fw.op("dve", lambda: nc.vector.tensor_tensor(out=ost[o][oh:oh + 64, :], in0=ot[oh:oh + 64, :],
                                                                  in1=rd[oh:oh + 64, :], op=ALU.mult),
                          [otr, rdr], [R["ost4_%d" % o]])
                    fw.dma("pool", dst_dram, ost[o][oh:oh + 64, :], reads=[R["ost4_%d" % o]], writes=[R["O1s"]])
                pipe.defer(10, stage2)

            seqs = [(4096, "p", 0, 32), (8192, "p", 4096, 32), (0, "s", 0, 128)]
            for (qbase, kind, kbase, nkt) in seqs:
                for g in range(4):
                    for e_ in range(2):
                        if kind == "p":
                            fw.dma("sp", K1b[e_ * 64:e_ * 64 + 64, e_, 0:4096], K1p[g, :, kbase:kbase + 4096],
                                   writes=[R["K1b"]])
                        else:
                            for r_ in range(4):
                                fw.dma("sp", K1b[e_ * 64:e_ * 64 + 64, e_, r_ * 4096:(r_ + 1) * 4096],
                                       K1g[g][r_ * 64:r_ * 64 + 64, :],
                                       reads=[R["K1g%d" % g]], writes=[R["K1b"]])
                    vres = [] if kind == "p" else [R["V1g%d" % g]]
                    for par in range(2):
                        for t0 in range(0, nkt, 16):
                            if kind == "p":
                                src = V1p[kbase + t0 * 128:kbase + (t0 + 16) * 128, g * 64:g * 64 + 64]
                            else:
                                src = V1g[g][kbase + t0 * 128:kbase + (t0 + 16) * 128, :]
                            fw.dma("sp", V1b[:, par, t0:t0 + 16, par * 64:par * 64 + 64],
                                   src.rearrange("(c k) f -> k c f", k=128), reads=vres, writes=[R["V1b_%d" % par]])
                    for qc in (2 * g, 2 * g + 1):
                        qb_i = ctr["q"] % 2
                        ctr["q"] += 1
                        Qb, Qres = Qb1[qb_i], R["Qb1_%d" % qb_i]
                        fw.dma("sp", Qb[:], Q1s[qc, :, qbase:qbase + 4096], writes=[Qres])
                        for e in range(2):
                            for m in range(8):
                                ob = ctr["ob"] % 2
                                ctr["ob"] += 1
                                dst = O1s[qc, e * 64:e * 64 + 64, qbase + m * 512:qbase + (m + 1) * 512]
                                for c2 in range(nkt // 2):
                                    st_i = ctr["st"] % 2
                                    pt_i = ctr["st"] % 3
                                    ctr["st"] += 1

                                    def qk(c2=c2, st_i=st_i, pt_i=pt_i, e=e, m=m, Qb=Qb, Qres=Qres):
                                        for u in range(2):
                                            c = 2 * c2 + u
                                            mm(STP[st_i][:, u * 512:(u + 1) * 512], K1b[:, e, c * 128:(c + 1) * 128],
                                               Qb[:, m * 512:(m + 1) * 512], True, True, [R["K1b"], Qres], [STR[st_i]])
                                        fw.op("act", lambda: nc.scalar.activation(
                                            out=PT[:, 2 * pt_i:2 * pt_i + 2, :].rearrange("p a b -> p (a b)"), in_=STP[st_i][:, :],
                                            func=AF.Exp, scale=0.125),
                                            [STR[st_i]], [R["PT4_%d" % pt_i]])

                                    def pv(c2=c2, pt_i=pt_i, e=e, ob=ob, dst=dst, nkt=nkt):
                                        for u in range(2):
                                            c = 2 * c2 + u
                                            mm(banks[4 + ob][:, :], V1b[:, e, c, :], PT[:, 2 * pt_i + u, :], c == 0, c == nkt - 1,
                                               [R["V1b_%d" % e], R["PT4_%d" % pt_i]], [BK[4 + ob]])
                                        if c2 == nkt // 2 - 1:
                                            finalize4(ob, e, dst)
                                    pipe.add(qk, pv)
                        pipe.run(look=2)
            pipe.flush()
        fw.barrier()

        with contextlib.ExitStack() as ph:
            pfx = "p5"
            stg = sbuf(ph, "p5stg", [128, D], F32)
            stgres = R["p5stg"]
            W2r = sbuf(ph, "p5W2r", [128, NJ, D], BF16)
            load_W2r(ph, 1, W2r, stg, stgres, pfx)
            Woc = sbuf(ph, "p5Woc", [128, 8, D], BF16)
            for kc in range(8):
                load_cast(stg, stgres, w_out_c[kc * 128:(kc + 1) * 128, :], D, [(0, Woc[:, kc, :], D, R["p5Woc"])])
            gfin = sbuf(ph, "gfin", [128, D], F32)
            fw.dma("sp", gfin[:], final_norm.partition_broadcast(128), writes=[R["gfin"]])
            w13 = [sbuf(ph, "p5w13_%d" % i, [128, 8, 256], BF16) for i in range(3)]
            silu_t = [sbuf(ph, "p5silu%d" % i, [128, 512], F32) for i in range(2)]
            gT = sbuf(ph, "p5gT", [128, NJ, 512], BF16)
            oT = [sbuf(ph, "p5oT%d" % i, [128, 8, 512], BF16) for i in range(2)]
            xtg5 = [sbuf(ph, "p5xtg%d" % i, [128, 4, D], F32) for i in range(2)]
            junk = sbuf(ph, "p5junk", [128, D], F32)
            xs4 = [sbuf(ph, "p5xs4_%d" % i, [128, D], BF16) for i in range(4)]
            ssq = [sbuf(ph, "p5ss%d" % i, [128, 8], F32) for i in range(2)]
            xnT = [sbuf(ph, "p5xnT%d" % i, [128, 8, 512], BF16) for i in range(1)]
            yt = [sbuf(ph, "p5y%d" % i, [128, D], F32) for i in range(2)]
            ny = 0
            NG5 = NOWN // 512

            def p5_bufs(gi):
                xt4 = [xtg5[gi % 2][:, t, :] for t in range(4)]
                xres4 = [R["p5x%d" % ((gi % 2) * 4 + t)] for t in range(4)]
                return oT[gi % 2], R["p5oT%d" % (gi % 2)], xt4, xres4, ssq[gi % 2], R["p5ss%d" % (gi % 2)]

            def p5_load(gi):
                o_t, o_r, xt4, xres4, sq, sqr = p5_bufs(gi)
                ob0 = gi * 512
                fw.dma("sp", o_t[:, :, :], O1s[:, :, ob0:ob0 + 512].rearrange("c p t -> p c t"), writes=[o_r])
                fw.dma("sp", xtg5[gi % 2][:, :, :], X1s[ob0:ob0 + 512, :].rearrange("(t p) f -> p t f", p=128), writes=xres4)

            def p5_ap_tile(gi, t):
                o_t, o_r, xt4, xres4, sq, sqr = p5_bufs(gi)
                outproj_residual(o_t, o_r, 8, Woc, R["p5Woc"], xt4, xres4, tiles=(t,))
                sumsq(xt4[t][:], xres4[t], junk[:], R["p5junk"], sq[:, t:t + 1], sqr)
                rstd_from_ss(sq[:, t:t + 1], sqr, sq[:, t:t + 1], sqr, D)
                scale_only(xt4[t][:], xres4[t], sq[:, t:t + 1], sqr, xs4[t], R["p5xs4_%d" % t])

            def p5_ap_tr(gi, t):
                transpose_only(xs4[t], R["p5xs4_%d" % t], xnT[0], R["p5xnT0"], t * 128)

            p5_load(0)
            for j_ in range(3):
                w13_load(1, j_, w13, pfx)
            for t in range(4):
                p5_ap_tile(0, t)
            for t in range(4):
                p5_ap_tr(0, t)
            for gi in range(NG5):
                ob0 = gi * 512
                o_t, o_r, xt4, xres4, sq, sqr = p5_bufs(gi)
                def after_up5(gi=gi):
                    if gi + 1 < NG5:
                        for j_ in range(3):
                            w13_load(1, j_, w13, pfx)

                def hook(slot, gi=gi):
                    if slot == 8 and gi + 1 < NG5:
                        p5_load(gi + 1)
                    if gi + 1 >= NG5 or slot < NJ:
                        return
                    k = slot - NJ
                    if k < 4:
                        p5_ap_tile(gi + 1, k)
                    else:
                        p5_ap_tr(gi + 1, k - 4)
                ffn_block(ph, 1, xt4, xres4, xnT[0], R["p5xnT0"], W2r, gT, w13, pfx, hook=hook, down_banks=2, after_up=after_up5)
                for t in range(4):
                    sumsq(xt4[t][:], xres4[t], junk[:], R["p5junk"], sq[:, 4 + t:5 + t], sqr)
                    rstd_from_ss(sq[:, 4 + t:5 + t], sqr, sq[:, 4 + t:5 + t], sqr, D)
                    y_t, y_r = yt[ny % 2], R["p5y%d" % (ny % 2)]
                    ny += 1
                    fw.op("dve", lambda: nc.vector.scalar_tensor_tensor(out=y_t[:], in0=xt4[t][:], scalar=sq[:, 4 + t:5 + t],
                                                                         in1=gfin[:], op0=ALU.mult, op1=ALU.mult),
                          [xres4[t], sqr, R["gfin"]], [y_r])
                    fw.dma("pool", yout[ob0 + t * 128:ob0 + (t + 1) * 128, :], y_t[:], reads=[y_r], writes=[R["yout"]])
        fw.barrier()
    return nc, fw


_CACHE = {}


def _tables():
    if "t" in _CACHE:
        return _CACHE["t"]
    import ml_dtypes
    k = np.arange(128)[:, None].astype(np.float64)
    slopesA = 2.0 ** (-8.0 * np.arange(1, 9) / 8)
    q = np.arange(384)[None, :].astype(np.float64)
    relA = np.abs(k - q + 128)
    biasA = np.zeros((128, 8 * 384), np.float32)
    for h in range(8):
        b = np.where(relA <= 128, -slopesA[h] * relA * 8.0, NEGM * 8.0)
        biasA[:, h * 384:(h + 1) * 384] = b
    slopesB = 2.0 ** (-8.0 * np.arange(1, 13) / 12)
    q = np.arange(256)[None, :].astype(np.float64)
    relB = np.abs(k - q + 64)
    biasB = np.zeros((128, 12 * 512), np.float32)
    dil = (1, 4, 16)
    for hidx in range(12):
        d = dil[hidx // 4]
        b = np.where(relB <= 64, -np.float32(slopesB[hidx]).astype(np.float64) * d * relB * 8.0, NEGM * 8.0)
        hi = b.astype(np.float32).astype(ml_dtypes.bfloat16).astype(np.float32)
        lo = (b - hi).astype(np.float32).astype(ml_dtypes.bfloat16).astype(np.float32)
        lo = np.where(relB <= 64, lo, 0.0)
        biasB[:, hidx * 512:hidx * 512 + 256] = hi
        biasB[:, hidx * 512 + 256:hidx * 512 + 512] = lo
    ident = np.eye(128, dtype=np.float32)
    _CACHE["t"] = (biasA, biasB, ident)
    return _CACHE["t"]


def _rope_tables(positions):
    n_freq = 16
    inv = (10000.0 ** (-np.arange(n_freq, dtype=np.float32) / n_freq)).astype(np.float32)
    row = (positions // 64).astype(np.float32)
    col = (positions % 64).astype(np.float32)
    ang = np.concatenate([row[:, None] * inv, col[:, None] * inv], axis=-1).astype(np.float32)
    return np.cos(ang).astype(np.float32), np.sin(ang).astype(np.float32)


def kernel(x_prompt, x_sample, norm_mix, w_in_ab, w_out_ab, sink_a, w_in_c, w_out_c, q_gain_c, k_gain_c,
           norm_ffn, ffn_w1, ffn_w3, ffn_w2, final_norm):
    f32 = lambda a: np.ascontiguousarray(np.asarray(a, dtype=np.float32))
    x_prompt, x_sample = f32(x_prompt), f32(x_sample)
    biasA, biasB, ident = _tables()
    if "nc" not in _CACHE:
        _CACHE["nc"] = build_program()
    nc, fw = _CACHE["nc"]
    shared = {
        "biasA": biasA, "biasB": biasB, "ident": ident,
        "norm_mix": f32(norm_mix), "w_in_ab": f32(w_in_ab)[0], "w_out_ab": f32(w_out_ab)[0], "sink_a": f32(sink_a),
        "w_in_c": f32(w_in_c)[0], "w_out_c": f32(w_out_c)[0], "q_gain_c": f32(q_gain_c), "k_gain_c": f32(k_gain_c),
        "norm_ffn": f32(norm_ffn), "ffn_w1": f32(ffn_w1), "ffn_w3": f32(ffn_w3), "ffn_w2": f32(ffn_w2),
        "final_norm": f32(final_norm).reshape(1, D),
    }
    in_maps = []
    for c in range(8):
        sq, qq = c // 4, c % 4
        xs = x_sample[sq]
        xin = np.zeros((NT0, D), np.float32)
        if qq > 0:
            xin[0:1024] = xs[qq * SEGL - 1024:qq * SEGL]
        xin[1024:5120] = xs[qq * SEGL:(qq + 1) * SEGL]
        if qq < 3:
            xin[5120:6144] = xs[(qq + 1) * SEGL:(qq + 1) * SEGL + 1024]
        xin[6144:10240] = x_prompt[2 * c]
        xin[10240:14336] = x_prompt[2 * c + 1]
        masks = np.zeros((128, 12), np.float32)
        for si in range(3):
            lv, rv = (qq != 0, qq != 3) if si == 0 else (False, False)
            if not lv:
                masks[0:64, si * 4 + 0] = NEGM
                masks[:, si * 4 + 2] = NEGM
            if not rv:
                masks[64:128, si * 4 + 1] = NEGM
                masks[:, si * 4 + 3] = NEGM
        pos = np.concatenate([np.arange(4096) + qq * SEGL, np.arange(4096), np.arange(4096)])
        cosT, sinT = _rope_tables(pos)
        m = dict(shared)
        m.update({"xin": xin, "masks": masks, "cosT": cosT, "sinT": sinT})
        in_maps.append(m)
    if _CACHE.get('debug_maps'):
        return in_maps
    res = run_bass_kernel_spmd(nc, in_maps, core_ids=list(range(8)))
    y_prompt = np.zeros((16, 4096, D), np.float32)
    y_sample = np.zeros((2, 16384, D), np.float32)
    for c in range(8):
        y = np.asarray(res.results[c]["yout"], dtype=np.float32)
        sq, qq = c // 4, c % 4
        y_sample[sq, qq * SEGL:(qq + 1) * SEGL] = y[0:4096]
        y_prompt[2 * c] = y[4096:8192]
        y_prompt[2 * c + 1] = y[8192:12288]
    return (y_prompt, y_sample)
```
